# Optimizing a Trainium2 kernel written in Bass

```python
import math
import jax
import jax.numpy as jnp
from jax import lax
import numpy as np


D_MODEL = 1024
BATCH = 16
SEQ = 2048
DEPTH = 2

N_A = (DEPTH + 1) // 2
N_B = DEPTH - N_A
POOL_WINDOWS = (2, 4, 8, 16)
N_POOL_GROUPS = len(POOL_WINDOWS)
POOL_GROUP = D_MODEL // N_POOL_GROUPS
HEAD_DIM = 64
N_HEADS = D_MODEL // (2 * HEAD_DIM)
V_DIM = 2 * HEAD_DIM
QK_WIDTH = 2 * N_HEADS * HEAD_DIM
V_WIDTH = N_HEADS * V_DIM
D_FF = 2816
CONV_WIDTH = 3
Q_BLOCK = 128
EPS = 1e-6

kernel_name = "yoco_pool_diffattn_convffn"


def rmsnorm(x, g):
    xf = x.astype(jnp.float32)
    y = xf * lax.rsqrt(jnp.mean(xf * xf, axis=-1, keepdims=True) + EPS)
    return (y * g.astype(jnp.float32)).astype(x.dtype)


def lambda_init_fn(layer_idx):
    return 0.8 - 0.6 * math.exp(-0.3 * layer_idx)


def multiscale_pool(h, w_pool, scale):
    B, S, D = h.shape
    hf = h.astype(jnp.float32)
    cs = jnp.concatenate([jnp.zeros((B, 1, D), jnp.float32), jnp.cumsum(hf, axis=1)], axis=1)
    t = jnp.arange(S)
    diffs = []
    for g, w in enumerate(POOL_WINDOWS):
        sl = slice(g * POOL_GROUP, (g + 1) * POOL_GROUP)
        lo = jnp.maximum(t + 1 - w, 0)
        cnt = jnp.minimum(t + 1, w).astype(jnp.float32)[None, :, None]
        mean = (cs[:, 1:, sl] - cs[:, lo, sl]) / cnt
        diffs.append(mean - hf[:, :, sl])
    d = jnp.stack(diffs, axis=2)
    y = jnp.einsum('bsgc,gce->bsge', d, w_pool.astype(jnp.float32)).reshape(B, S, D)
    return (y * scale.astype(jnp.float32)).astype(h.dtype)


def causal_dwconv(u, w, b):
    C = u.shape[-1]
    y = lax.conv_general_dilated(
        u, w[:, None, :].astype(u.dtype), window_strides=(1,),
        padding=[(CONV_WIDTH - 1, 0)], dimension_numbers=('NWC', 'WIO', 'NWC'),
        feature_group_count=C)
    return y + b.astype(u.dtype)


def conv_ffn(h, w_up, conv_w, conv_b, w_down):
    u = h @ w_up
    u = causal_dwconv(u, conv_w, conv_b)
    a, g = jnp.split(u, 2, axis=-1)
    return (jax.nn.silu(g) * a) @ w_down


def diff_attention(h, wq, lam, subln_g, wo, k, v, lam_init):
    B, S, _ = h.shape
    nblk = S // Q_BLOCK
    q = (h @ wq).reshape(B, S, 2, N_HEADS, HEAD_DIM)
    lamf = lam.astype(jnp.float32)
    lam_full = (jnp.exp(jnp.sum(lamf[0] * lamf[1])) - jnp.exp(jnp.sum(lamf[2] * lamf[3])) + lam_init)
    qb = jnp.moveaxis(q.reshape(B, nblk, Q_BLOCK, 2, N_HEADS, HEAD_DIM), 1, 0)
    kpos = jnp.arange(S)
    sm_scale = HEAD_DIM ** -0.5

    def block(args):
        qi, i = args
        s = jnp.einsum('bqchd,bkchd->bchqk', qi, k).astype(jnp.float32) * sm_scale
        qpos = i * Q_BLOCK + jnp.arange(Q_BLOCK)
        mask = kpos[None, :] <= qpos[:, None]
        s = jnp.where(mask, s, jnp.finfo(jnp.float32).min)
        p = jax.nn.softmax(s, axis=-1)
        attn = p[:, 0] - lam_full * p[:, 1]
        return jnp.einsum('bhqk,bkhe->bqhe', attn.astype(v.dtype), v)

    o = lax.map(block, (qb, jnp.arange(nblk)))
    o = jnp.moveaxis(o, 0, 1).reshape(B, S, N_HEADS, V_DIM)
    o = rmsnorm(o, subln_g) * (1.0 - lam_init)
    return o.reshape(B, S, V_WIDTH) @ wo


def setup_inputs(seed: int = 0) -> dict:
    key = jax.random.key(seed)
    ks = jax.random.split(key, 20)
    f32 = jnp.float32
    nrm = lambda k, shape, s: jax.random.normal(k, shape, f32) * s
    return {
        'x': jax.random.normal(ks[0], (BATCH, SEQ, D_MODEL), f32),
        'norm_g': 1.0 + nrm(ks[1], (DEPTH, 2, D_MODEL), 0.02),
        'pool_w': nrm(ks[2], (N_A, N_POOL_GROUPS, POOL_GROUP, POOL_GROUP), POOL_GROUP ** -0.5),
        'pool_scale': 1.0 + nrm(ks[3], (N_A, D_MODEL), 0.02),
        'kv_norm': 1.0 + nrm(ks[4], (D_MODEL,), 0.02),
        'w_kv': nrm(ks[5], (D_MODEL, QK_WIDTH + V_WIDTH), D_MODEL ** -0.5),
        'wq': nrm(ks[6], (N_B, D_MODEL, QK_WIDTH), D_MODEL ** -0.5),
        'lam': nrm(ks[7], (N_B, 4, HEAD_DIM), 0.1),
        'subln_g': 1.0 + nrm(ks[8], (N_B, V_DIM), 0.02),
        'wo': nrm(ks[9], (N_B, V_WIDTH, D_MODEL), V_WIDTH ** -0.5),
        'w_up': nrm(ks[10], (DEPTH, D_MODEL, 2 * D_FF), D_MODEL ** -0.5),
        'conv_w': nrm(ks[11], (DEPTH, CONV_WIDTH, 2 * D_FF), CONV_WIDTH ** -0.5),
        'conv_b': nrm(ks[12], (DEPTH, 2 * D_FF), 0.01),
        'w_down': nrm(ks[13], (DEPTH, D_FF, D_MODEL), D_FF ** -0.5),
        'final_norm': 1.0 + nrm(ks[14], (D_MODEL,), 0.02),
    }


def reference(x, norm_g, pool_w, pool_scale, kv_norm, w_kv, wq, lam, subln_g, wo,
              w_up, conv_w, conv_b, w_down, final_norm):
    B, S, _ = x.shape
    k = v = None
    for i in range(DEPTH):
        if i < N_A:
            x = x + multiscale_pool(rmsnorm(x, norm_g[i, 0]), pool_w[i], pool_scale[i])
        else:
            if i == N_A:
                kv = rmsnorm(x, kv_norm) @ w_kv
                k = kv[..., :QK_WIDTH].reshape(B, S, 2, N_HEADS, HEAD_DIM)
                v = kv[..., QK_WIDTH:].reshape(B, S, N_HEADS, V_DIM)
            j = i - N_A
            x = x + diff_attention(rmsnorm(x, norm_g[i, 0]), wq[j], lam[j], subln_g[j], wo[j],
                                   k, v, lambda_init_fn(i))
        x = x + conv_ffn(rmsnorm(x, norm_g[i, 1]), w_up[i], conv_w[i], conv_b[i], w_down[i])
    return rmsnorm(x, final_norm)
```

```python
from contextlib import ExitStack

import concourse.bass as bass
import concourse.mybir as mybir

F32 = mybir.dt.float32
BF16 = mybir.dt.bfloat16
ALU = mybir.AluOpType
AF = mybir.ActivationFunctionType

ENGINES = ("pe", "act", "dve", "pool", "sp")
TRACE_OPS = None


class Buf:
    def __init__(self, S, name, shape, dtype, space="sbuf", arena=None, off=0):
        self.S = S
        self.name = name
        self.shape = tuple(shape)
        self.dtype = dtype
        self.esz = 2 if dtype == F32 else 1
        self.strides = []
        st = 1
        for d in reversed(self.shape):
            self.strides.insert(0, st)
            st *= d
        self.size = st
        self.recs = []
        self.alias = []
        self.is_psum = (space == "psum")
        if arena is None:
            full = [128] + list(shape)
            if space == "sbuf":
                self.h = S.stack.enter_context(S.nc.sbuf_tensor(name, full, dtype))
            else:
                self.h = S.stack.enter_context(S.nc.psum_tensor(name, full, dtype))
            self.base = 0
            self.root = self.h
            self.carved = []
        else:
            assert off % 4 == 0
            nbytes = self.size * self.esz * 2
            assert nbytes % 4 == 0 and off + nbytes <= arena.size * 4, (name, off, nbytes)
            w0 = off // 4
            ap = arena.h[:, w0:w0 + nbytes // 4]
            if dtype != F32:
                ap = ap.bitcast(dtype)
            if len(self.shape) > 1:
                names = " ".join("d%d" % i for i in range(len(self.shape)))
                kw = {"d%d" % i: self.shape[i] for i in range(len(self.shape) - 1)}
                ap = ap.rearrange("p (%s) -> p %s" % (names, names), **kw)
            self.root = ap
            self.base = off // 2
            self.end = self.base + self.size * self.esz
            for ob in arena.carved:
                if ob.base < self.end and self.base < ob.end:
                    ob.alias.append(self)
                    self.alias.append(ob)
            arena.carved.append(self)

    def v(self, *idx, p=None):
        idx = list(idx) + [slice(None)] * (len(self.shape) - len(idx))
        norm = []
        for d, i in zip(self.shape, idx):
            if isinstance(i, int):
                norm.append((i, i + 1, True))
            else:
                a = 0 if i.start is None else i.start
                b = d if i.stop is None else i.stop
                assert 0 <= a < b <= d, (self.name, idx)
                norm.append((a, b, False))
        n = len(norm)
        j = n - 1
        while j > 0 and norm[j][0] == 0 and norm[j][1] == self.shape[j]:
            j -= 1
        offs = [0]
        for d in range(j):
            a, b, _ = norm[d]
            offs = [o + i * self.strides[d] for o in offs for i in range(a, b)]
        a, b, _ = norm[j]
        e = self.esz
        ranges = [(self.base + (o + a * self.strides[j]) * e, self.base + (o + b * self.strides[j]) * e) for o in offs]
        if self.is_psum:
            BK = 1024
            ranges = sorted(set((lo // BK * BK, (hi + BK - 1) // BK * BK) for (lo, hi) in ranges))
        sl = [slice(None) if p is None else slice(p[0], p[1])]
        for (a, b, isint) in norm:
            sl.append(a if isint else slice(a, b))
        ap = self.root[tuple(sl)]
        return View(self, ap, ranges)


class View:
    def __init__(self, buf, ap, ranges):
        self.buf = buf
        self.ap = ap
        self.ranges = ranges


class Op:
    __slots__ = ("eng", "emit", "deps", "signal", "val", "semkey", "inc", "idx", "name")

    def __init__(self, eng, emit, name=""):
        self.eng = eng
        self.emit = emit
        self.deps = {}
        self.signal = False
        self.val = None
        self.semkey = None
        self.inc = 1
        self.name = name


class Sched:
    def __init__(self, nc, stack):
        self.nc = nc
        self.stack = stack
        self.ops = {e: [] for e in ENGINES}
        self.dma_groups = {}
        self.all_ops = []

    def sbuf(self, name, shape, dtype):
        return Buf(self, name, shape, dtype, "sbuf")

    def psum(self, name, shape, dtype=F32):
        return Buf(self, name, shape, dtype, "psum")

    def carve(self, arena, name, shape, dtype, off):
        return Buf(self, name, shape, dtype, "sbuf", arena=arena, off=off)

    def _track(self, op, reads, writes):
        for vw in reads:
            for b in [vw.buf] + vw.buf.alias:
                if not b.recs:
                    continue
                for (lo, hi) in vw.ranges:
                    for (rlo, rhi, rop, isw) in b.recs:
                        if rlo < hi and lo < rhi:
                            if isw:
                                op.deps[rop] = True
                            elif b.is_psum and rop.eng != op.eng:
                                op.deps.setdefault(rop, False)
        for vw in writes:
            for b in [vw.buf] + vw.buf.alias:
                if not b.recs:
                    continue
                for (lo, hi) in vw.ranges:
                    for (rlo, rhi, rop, isw) in b.recs:
                        if rlo < hi and lo < rhi and rop is not op:
                            op.deps.setdefault(rop, False)
        for vw in writes:
            for b in [vw.buf] + vw.buf.alias:
                for (lo, hi) in vw.ranges:
                    if b.recs:
                        b.recs = [r for r in b.recs if not (r[0] >= lo and r[1] <= hi)]
            for (lo, hi) in vw.ranges:
                vw.buf.recs.append((lo, hi, op, True))
        for vw in reads:
            b = vw.buf
            for (lo, hi) in vw.ranges:
                if op.semkey is None:
                    b.recs = [r for r in b.recs if not (r[0] == lo and r[1] == hi and (not r[3]) and r[2].eng == op.eng and r[2].semkey is None)]
                b.recs.append((lo, hi, op, False))

    def op(self, eng, emit, reads=(), writes=(), name=""):
        o = Op(eng, emit, name)
        self._track(o, reads, writes)
        o.deps.pop(o, None)
        self.ops[eng].append(o)
        self.all_ops.append(o)
        return o

    def dma(self, eng, out_ap, in_ap, group, reads=(), writes=(), name=""):
        def emit(e, out_ap=out_ap, in_ap=in_ap):
            return e.dma_start(out=out_ap, in_=in_ap)
        o = Op(eng, emit, name)
        o.semkey = ("dma", group)
        n = self.dma_groups.get(group, 0) + 1
        self.dma_groups[group] = n
        o.val = 16 * n
        o.inc = 16
        o.signal = True
        self._track(o, reads, writes)
        o.deps.pop(o, None)
        self.ops[eng].append(o)
        self.all_ops.append(o)
        return o

    def emit_all(self, final_wait_ops=()):
        nc = self.nc
        for o in self.all_ops:
            for d in o.deps:
                if d.semkey is None:
                    if d.eng == o.eng and o.semkey is None and d.eng == "pe":
                        continue
                    d.signal = True
        for o in final_wait_ops:
            if o.semkey is None:
                o.signal = True
        counts = {}
        for e in ENGINES:
            c = 0
            for o in self.ops[e]:
                if o.semkey is None and o.signal:
                    c += 1
                    o.val = c
            counts[e] = c
        sems = {}
        for e in ENGINES:
            if counts[e] > 0:
                sems[("eng", e)] = self.stack.enter_context(nc.semaphore("s_" + e))
        for g in self.dma_groups:
            sems[("dma", g)] = self.stack.enter_context(nc.semaphore("d_" + str(g)))
        self.nsems = len(sems)

        def key_of(d):
            return d.semkey if d.semkey is not None else ("eng", d.eng)

        eng_obj = {"pe": "tensor", "act": "scalar", "dve": "vector", "pool": "gpsimd", "sp": "sync"}
        nwaits = {e: 0 for e in ENGINES}

        def body(ename, extra_final=None):
            def fn(eng):
                seen = {}
                for o in self.ops[ename]:
                    need = {}
                    for d in o.deps:
                        if d.semkey is None and d.eng == ename and o.semkey is None and ename == "pe":
                            continue
                        k = key_of(d)
                        if seen.get(k, 0) >= d.val:
                            continue
                        if need.get(k, 0) < d.val:
                            need[k] = d.val
                    if TRACE_OPS is not None:
                        TRACE_OPS.append((ename, len(TRACE_OPS), o.emit.__code__.co_firstlineno, dict(need), (key_of(o), o.val) if o.signal else None))
                    for k, v in need.items():
                        eng.wait_ge(sems[k], v)
                        seen[k] = v
                        nwaits[ename] += 1
                    ins = o.emit(eng)
                    if o.signal:
                        ins.then_inc(sems[key_of(o)], o.inc)
                if extra_final:
                    for d in extra_final:
                        k = key_of(d)
                        if seen.get(k, 0) < d.val:
                            eng.wait_ge(sems[k], d.val)
                            seen[k] = d.val
            return fn

        with nc.Block() as block:
            for e in ENGINES:
                if not self.ops[e] and not (e == "sp" and final_wait_ops):
                    continue
                dec = getattr(block, eng_obj[e])
                dec(body(e, final_wait_ops if e == "sp" else None))
        self.nwaits = nwaits

import math, os
KDBG = os.environ.get('KDBG', '')
import numpy as np
from concourse.bass_utils import run_bass_kernel_spmd

D = 1024
SEQ = 2048
NSEQ = 2
DFF = 2816
NFC = 22
EPS = 1e-6
LAM_INIT = 0.8 - 0.6 * math.exp(-0.3 * 1)
SM_SCALE = 0.125
NEG = -30000.0

C_NG, C_KV, C_FIN, C_PS, C_SUB, C_CW, C_CB, C_LAM, C_INV, NV = 0, 32, 40, 48, 56, 57, 321, 409, 665, 697


def build_program(dbg=None, stop=None):
    nc = bass.Bass("TRN2", target_bir_lowering=False)
    dt_in = lambda n, s: nc.dram_tensor(n, s, F32, kind="ExternalInput").ap()
    xT_d = dt_in("xT", [NSEQ, 8, 128, SEQ])
    vec_d = dt_in("vec", [128, NV])
    cst_d = dt_in("cst", [128, 11, 128])
    wup_d = dt_in("wup", [2, NFC, 128, 8, 256])
    wdn_d = dt_in("wdn", [2, 8, 128, NFC, 128])
    pw_d = dt_in("pw", [128, 4, 2, 256])
    wk_d = dt_in("wk", [4, 128, 2, 8, 128])
    wv_d = dt_in("wv", [4, 128, 8, 256])
    wq_d = dt_in("wq", [4, 128, 2, 8, 128])
    wo_d = dt_in("wo", [4, 128, 2, 8, 128])
    out_d = nc.dram_tensor("outT", [NSEQ, 8, 128, SEQ], F32, kind="ExternalOutput").ap()
    dbg_d = {}
    if dbg:
        for k, shp in dbg.items():
            dbg_d[k] = nc.dram_tensor("dbg_" + k, shp, F32, kind="ExternalOutput").ap()

    with ExitStack() as st:
        S = Sched(nc, st)
        xT = S.sbuf("xTs", [8, SEQ], F32)
        vec = S.sbuf("vecs", [NV], F32)
        vec2 = S.sbuf("vec2", [64], F32)
        cst = S.sbuf("csts", [11, 128], BF16)
        sq = S.sbuf("sqs", [8, 512], BF16)
        rstd = [S.sbuf("rstd%d" % i, [512], F32) for i in range(2)]
        uh = S.sbuf("uh", [2 * NFC, 2], F32)
        hhalo = S.sbuf("hhalo", [8, 16], F32)
        kq2 = S.sbuf("kq2", [40], F32)

        ARENA_BYTES = 124 * 1024
        arena = S.sbuf("arena", [ARENA_BYTES // 4], F32)
        PS = S.psum("PS", [4096], F32)

        class Bump:
            def __init__(self):
                self.off = 0

            def take(self, name, shape, dtype):
                n = 1
                for d in shape:
                    n *= d
                nb = n * (4 if dtype == F32 else 2)
                nb = (nb + 31) // 32 * 32
                b = S.carve(arena, name, shape, dtype, self.off)
                self.off += nb
                assert self.off <= ARENA_BYTES, (name, self.off)
                return b

        bp = Bump()
        hb = [bp.take("hb%d" % i, [8, 528], F32) for i in range(2)]
        tA = bp.take("tA", [2, 528], F32)
        tB = bp.take("tB", [2, 528], F32)
        tC = bp.take("tC", [2, 528], F32)
        tD = bp.take("tD", [2, 528], F32)
        dT = [bp.take("dT%d" % i, [8, 512], BF16) for i in range(2)]
        hb16 = [bp.take("hb16_%d" % i, [8, 528], BF16) for i in range(2)]
        hfx = bp.take("hfx", [8, 32], F32)
        pwb = bp.take("pwb", [4, 2, 256], BF16)
        bf_ = Bump()
        hT = bf_.take("hT", [8, 1024], BF16)
        actb = bf_.take("actb", [NFC, 1024], BF16)
        ya = [bf_.take("ya%d" % i, [1024], F32) for i in range(3)]
        yg = [bf_.take("yg%d" % i, [1024], F32) for i in range(2)]
        sg = [bf_.take("sg%d" % i, [1024], F32) for i in range(3)]
        NWUP = 5
        wupb = [bf_.take("wupb%d" % i, [8, 256], BF16) for i in range(NWUP)]
        wdnb = [bf_.take("wdnb%d" % i, [NFC, 128], BF16) for i in range(2)]
        ba = Bump()
        KT = ba.take("KT", [8, SEQ], BF16)
        Vb = ba.take("Vb", [16, 1024], BF16)
        QT = ba.take("QT", [8, 512], BF16)
        hkv = ba.take("hkv", [8, 512], BF16)
        hq = ba.take("hq", [8, 512], BF16)
        PT = [[ba.take("PT%d%d" % (c, i), [512], BF16) for i in range(2)] for c in range(2)]
        ft = [ba.take("ft%d" % i, [512], F32) for i in range(4)]
        oT = ba.take("oT", [8, 512], BF16)
        wsm = [ba.take("wsm%d" % i, [2, 8, 128], BF16) for i in range(2)]
        wvb = [ba.take("wvb%d" % i, [8, 256], BF16) for i in range(2)]

        class _NS:
            def v(self, *a):
                return sq.v(7)
        nsq = _NS()
        bo = Bump()
        ost = [bo.take("ost%d" % i, [8, 512], F32) for i in range(1)]

        def apof(x):
            return x.ap if isinstance(x, View) else x

        def rd(*xs):
            return [x for x in xs if isinstance(x, View)]

        def mm(out, lhsT, rhs, start, stop):
            S.op("pe", lambda e: e.matmul(out.ap, lhsT=lhsT.ap, rhs=rhs.ap, start=start, stop=stop),
                 reads=[lhsT, rhs], writes=[out])

        def act(out, in_, func, bias=0.0, scale=1.0):
            if isinstance(bias, View) or isinstance(scale, View):
                f = lambda e: e.activation(out=out.ap, in_=in_.ap, func=func, bias=apof(bias), scale=apof(scale))
            else:
                f = lambda e: e.activation(out=out.ap, in_=in_.ap, func=func, bias=bias, scale=scale)
            return S.op("act", f, reads=[in_] + rd(bias, scale), writes=[out])

        def stt(out, in0, scalar, in1, op0, op1, eng="dve"):
            return S.op(eng, lambda e: e.scalar_tensor_tensor(out=out.ap, in0=in0.ap, scalar=apof(scalar), in1=in1.ap, op0=op0, op1=op1),
                        reads=[in0, in1] + rd(scalar), writes=[out])

        def tt(out, in0, in1, op, eng="dve"):
            return S.op(eng, lambda e: e.tensor_tensor(out=out.ap, in0=in0.ap, in1=in1.ap, op=op), reads=[in0, in1], writes=[out])

        def ts(out, in0, s1, s2, op0, op1, eng="dve"):
            return S.op(eng, lambda e: e.tensor_scalar(out=out.ap, in0=in0.ap, scalar1=apof(s1), scalar2=apof(s2), op0=op0, op1=op1),
                        reads=[in0] + rd(s1, s2), writes=[out])

        def cp(out, in_, eng="dve"):
            if eng == "act":
                return S.op("act", lambda e: e.copy(out=out.ap, in_=in_.ap), reads=[in_], writes=[out])
            return S.op(eng, lambda e: e.tensor_copy(out=out.ap, in_=in_.ap), reads=[in_], writes=[out])

        def memset(out, val, eng="dve"):
            return S.op(eng, lambda e: e.memset(out.ap, val), writes=[out])

        def vcol(c):
            return vec.v(slice(c, c + 1))

        def v2col(c):
            return vec2.v(slice(c, c + 1))

        ident = cst.v(0)
        negmask = cst.v(1)
        ones = cst.v(2)

        def dump(name, view):
            if name in dbg_d:
                S.dma("sp", dbg_d[name], view.ap, "dbg_" + name, reads=[view])

        rot = {"i": 0}

        def gps(n=512, banks=(0, 1, 2, 3, 4, 5, 6, 7)):
            b = banks[rot["i"] % len(banks)]
            rot["i"] += 1
            return PS.v(slice(512 * b, 512 * b + n))

        evq = {"i": 0}

        def evac_copy(out, in_):
            evq["i"] += 1
            cp(out, in_, eng="act" if evq["i"] % 2 else "dve")

        S.dma("sp", vec.v().ap, vec_d, "vec", writes=[vec.v()])
        S.dma("pool", cst.v().ap, cst_d, "cst", writes=[cst.v()])
        invc = S.sbuf("invc", [2, 16], F32)
        S.dma("sp", invc.v().ap, vec_d[:, C_INV:C_INV + 32].rearrange("p (a b) -> p a b", a=2), "invc", writes=[invc.v()])
        S.op("dve", lambda e: e.tensor_scalar_mul(out=vec2.v(slice(0, 48)).ap, in0=vec.v(slice(0, 48)).ap, scalar1=32.0),
             reads=[vec.v(slice(0, 48))], writes=[vec2.v(slice(0, 48))])
        S.op("dve", lambda e: e.tensor_scalar_mul(out=v2col(48).ap, in0=vcol(C_SUB).ap, scalar1=(1.0 - LAM_INIT) * math.sqrt(128.0)),
             reads=[vcol(C_SUB)], writes=[v2col(48)])
        l01 = rstd[0]
        tt(l01.v(slice(0, 64)), vec.v(slice(C_LAM, C_LAM + 64)), vec.v(slice(C_LAM + 64, C_LAM + 128)), ALU.mult)
        tt(l01.v(slice(64, 128)), vec.v(slice(C_LAM + 128, C_LAM + 192)), vec.v(slice(C_LAM + 192, C_LAM + 256)), ALU.mult)
        S.op("dve", lambda e: e.tensor_reduce(out=v2col(50).ap, in_=l01.v(slice(0, 64)).ap, axis=mybir.AxisListType.X, op=ALU.add),
             reads=[l01.v(slice(0, 64))], writes=[v2col(50)])
        S.op("dve", lambda e: e.tensor_reduce(out=v2col(51).ap, in_=l01.v(slice(64, 128)).ap, axis=mybir.AxisListType.X, op=ALU.add),
             reads=[l01.v(slice(64, 128))], writes=[v2col(51)])
        act(vec2.v(slice(52, 54)), vec2.v(slice(50, 52)), AF.Exp)
        tt(v2col(54), v2col(53), v2col(52), ALU.subtract)
        S.op("dve", lambda e: e.tensor_scalar_add(out=v2col(49).ap, in0=v2col(54).ap, scalar1=-LAM_INIT), reads=[v2col(54)], writes=[v2col(49)])
        memset(v2col(55), D * EPS)
        memset(v2col(56), 128.0 * EPS)
        eps_d = v2col(55)
        eps_h = v2col(56)
        neglam = v2col(49)
        gsub = v2col(48)

        def g32(idx):
            return v2col(idx)

        def norm_sq(t0):
            for c in range(8):
                act(sq.v(c), xT.v(c, slice(t0, t0 + 512)), AF.Square)

        def norm_fin(rbuf, psbank):
            pn = PS.v(slice(512 * psbank, 512 * psbank + 512))
            for c in range(8):
                mm(pn, ones, sq.v(c), c == 0, c == 7)
            act(rbuf.v(), pn, AF.Ln, bias=eps_d)
            act(rbuf.v(), rbuf.v(), AF.Exp, scale=-0.5)

        def norm_rstd(t0, rbuf, psbank):
            norm_sq(t0)
            norm_fin(rbuf, psbank)

        out_dmas = []

        def finish_early():
            od = [S.dma("sp", out_d[0, c], xT.v(c).ap, "out", reads=[xT.v(c)]) for c in range(8)]
            for o in od:
                o.val = od[-1].val
            out_dmas.extend(od)

        class _Stop(Exception):
            pass

        def chk(name):
            if stop == name:
                finish_early()
                raise _Stop()

        def load_x(s_, tiles=(0, 1, 2, 3)):
            for ti_ in tiles:
                xl = [S.dma("sp", xT.v(c, slice(512 * ti_, 512 * ti_ + 512)).ap, xT_d[s_, c, :, 512 * ti_:512 * ti_ + 512], "xin%d" % ti_,
                            writes=[xT.v(c, slice(512 * ti_, 512 * ti_ + 512))]) for c in range(8)]
                for o in xl:
                    o.val = xl[-1].val

        def p1_weights():
            S.dma("pool", pwb.v().ap, pw_d, "pw", writes=[pwb.v()])

        def p1_tile_pe(ti):
            t0 = 512 * ti
            hbc = hb16[ti % 2]
            if ti == 0:
                memset(hbc.v(slice(0, 8), slice(0, 16)), 0.0)
            else:
                cp(hbc.v(slice(0, 8), slice(0, 16)), hhalo.v(), eng="act")
            rb = rstd[ti % 2]
            norm_rstd(t0, rb, ti % 2)
            for c in range(8):
                stt(hbc.v(c, slice(16, 528)), xT.v(c, slice(t0, t0 + 512)), g32(C_NG + c), rb.v(), ALU.mult, ALU.mult)
            cp(hhalo.v(), hbc.v(slice(0, 8), slice(512, 528)), eng="act")
            dTc = dT[ti % 2]
            for gi, w in enumerate((2, 4, 8, 16)):
                for cc in (2 * gi, 2 * gi + 1):
                    pd = gps(512, banks=(2, 3, 4, 5, 6, 7))
                    for j in range(w):
                        mm(pd, cst.v(3 + 2 * gi + (1 if j else 0)), hbc.v(cc, slice(16 - j, 528 - j)), j == 0, j == w - 1)
                    cp(dTc.v(cc), pd, eng="act")
            if ti == 0:
                memset(hfx.v(slice(0, 8), slice(0, 16)), 0.0)
                for c in range(8):
                    stt(hfx.v(c, slice(16, 32)), xT.v(c, slice(0, 16)), g32(C_NG + c), rb.v(slice(0, 16)), ALU.mult, ALU.mult)
                for g, w in enumerate((2, 4, 8, 16)):
                    H = lambda lo, hi: hfx.v(slice(2 * g, 2 * g + 2), slice(lo, hi))
                    cur, other = tA, tB
                    tt(cur.v(slice(0, 2), slice(2, 32)), H(2, 32), H(1, 31), ALU.add)
                    sh = 2
                    while sh < w:
                        lo = 2 * sh
                        tt(other.v(slice(0, 2), slice(lo, 32)), cur.v(slice(0, 2), slice(lo, 32)), cur.v(slice(0, 2), slice(lo - sh, 32 - sh)), ALU.add)
                        cur, other = other, cur
                        sh *= 2
                    n = w - 1
                    tmp = other.v(slice(0, 2), slice(0, n))
                    tt(tmp, cur.v(slice(0, 2), slice(16, 16 + n)), invc.v(slice(0, 2), slice(0, n)), ALU.mult)
                    tt(dTc.v(slice(2 * g, 2 * g + 2), slice(0, n)), tmp, H(16, 16 + n), ALU.subtract)
            for oc in range(8):
                g, eh = oc // 2, oc % 2
                pp = gps(512, banks=(2, 3, 4, 5, 6, 7))
                for kc in range(2):
                    mm(pp, pwb.v(g, kc, slice(eh * 128, eh * 128 + 128)), dTc.v(2 * g + kc), kc == 0, kc == 1)
                stt(xT.v(oc, slice(t0, t0 + 512)), pp, vcol(C_PS + oc), xT.v(oc, slice(t0, t0 + 512)), ALU.mult, ALU.add)

        def p1_tile(ti):
            t0 = 512 * ti
            hbc = hb[ti % 2]
            if ti == 0:
                memset(hbc.v(slice(0, 8), slice(0, 16)), 0.0)
            else:
                cp(hbc.v(slice(0, 8), slice(0, 16)), hhalo.v(), eng="act")
            rb = rstd[ti % 2]
            norm_rstd(t0, rb, ti % 2)
            for c in range(8):
                stt(hbc.v(c, slice(16, 528)), xT.v(c, slice(t0, t0 + 512)), g32(C_NG + c), rb.v(), ALU.mult, ALU.mult)
            cp(hhalo.v(), hbc.v(slice(0, 8), slice(512, 528)), eng="act")
            dTc = dT[ti % 2]
            for g, w in ((3, 16), (0, 2), (2, 8), (1, 4)):
                pe_ = "dve"
                H = lambda lo, hi: hbc.v(slice(2 * g, 2 * g + 2), slice(lo, hi))
                cur, other = (tC, tD) if g >= 2 else (tA, tB)
                tt(cur.v(slice(0, 2), slice(2, 528)), H(2, 528), H(1, 527), ALU.add, eng=pe_)
                sh = 2
                while sh < w:
                    lo = 2 * sh
                    tt(other.v(slice(0, 2), slice(lo, 528)), cur.v(slice(0, 2), slice(lo, 528)), cur.v(slice(0, 2), slice(lo - sh, 528 - sh)), ALU.add, eng=pe_)
                    cur, other = other, cur
                    sh *= 2
                if pe_ == "dve":
                    stt(dTc.v(slice(2 * g, 2 * g + 2)), cur.v(slice(0, 2), slice(16, 528)), 1.0 / w, H(16, 528), ALU.mult, ALU.subtract)
                else:
                    tsm = other.v(slice(0, 2), slice(16, 528))
                    S.op("pool", lambda e, tsm=tsm, cur=cur, w=w: e.tensor_scalar_mul(out=tsm.ap, in0=cur.v(slice(0, 2), slice(16, 528)).ap, scalar1=1.0 / w),
                         reads=[cur.v(slice(0, 2), slice(16, 528))], writes=[tsm])
                    tt(dTc.v(slice(2 * g, 2 * g + 2)), tsm, H(16, 528), ALU.subtract, eng="pool")
                if ti == 0:
                    n = w - 1
                    tmp = other.v(slice(0, 2), slice(0, n))
                    tt(tmp, cur.v(slice(0, 2), slice(16, 16 + n)), invc.v(slice(0, 2), slice(0, n)), ALU.mult, eng=pe_)
                    tt(dTc.v(slice(2 * g, 2 * g + 2), slice(0, n)), tmp, H(16, 16 + n), ALU.subtract, eng=pe_)
            for oc in range(8):
                g, eh = oc // 2, oc % 2
                pp = gps(512, banks=(2, 3, 4, 5, 6, 7))
                for kc in range(2):
                    mm(pp, pwb.v(g, kc, slice(eh * 128, eh * 128 + 128)), dTc.v(2 * g + kc), kc == 0, kc == 1)
                stt(xT.v(oc, slice(t0, t0 + 512)), pp, vcol(C_PS + oc), xT.v(oc, slice(t0, t0 + 512)), ALU.mult, ALU.add)

        def final_tile(s, ti):
            t0 = 512 * ti
            rb = rstd[ti % 2]
            norm_rstd(t0, rb, ti % 2)
            ob = ost[0]
            for c in range(8):
                stt(ob.v(c), xT.v(c, slice(t0, t0 + 512)), g32(C_FIN + c), rb.v(), ALU.mult, ALU.mult)
            od = [S.dma("sp", out_d[s, c, :, t0:t0 + 512], ob.v(c).ap, "out", reads=[ob.v(c)]) for c in range(8)]
            for o in od:
                o.val = od[-1].val
            out_dmas.extend(od)
            if s + 1 < NSEQ:
                load_x(s + 1, (ti,))

        def _main():
          for s in range(NSEQ):
              if s == 0:
                  load_x(0)
              if stop == "load":
                  finish_early()
                  return
              if s == 0:
                  p1_weights()
                  for ti in range(4):
                      p1_tile_pe(ti)
              if s == 0:
                  dump("x1", xT.v())
              if stop == "p1":
                  finish_early()
                  break

              def ffn(l, between=None):
                  for sti in range(2):
                      T0 = 1024 * sti
                      for half in range(2):
                          rb = rstd[half]
                          norm_rstd(T0 + 512 * half, rb, half)
                          for c in range(8):
                              stt(hT.v(c, slice(512 * half, 512 * half + 512)), xT.v(c, slice(T0 + 512 * half, T0 + 512 * half + 512)),
                                  g32(C_NG + (l * 2 + 1) * 8 + c), rb.v(), ALU.mult, ALU.mult)
                      def load_wup(i):
                          S.dma("pool", wupb[i % NWUP].v().ap, wup_d[l, i], "wup%d" % (i % NWUP), writes=[wupb[i % NWUP].v()])

                      def load_w(idx):
                          if idx < NFC:
                              load_wup(idx)
                          elif idx < NFC + 2:
                              load_wdn(idx - NFC)

                      def load_wdn(j):
                          S.dma("pool", wdnb[j % 2].v().ap, wdn_d[l, j], "wdn%d" % (j % 2), writes=[wdnb[j % 2].v()])

                      DIST = NWUP - 1
                      prev_gate = [None]
                      for idx in range(DIST):
                          load_w(idx)
                      for i in range(NFC):
                          load_w(i + DIST)
                          wb = wupb[i % NWUP]
                          st_ = i % 2
                          pA = PS.v(slice(2048 * st_, 2048 * st_ + 1024))
                          pG = PS.v(slice(2048 * st_ + 1024, 2048 * st_ + 2048))
                          for part, pX in ((0, pA), (1, pG)):
                              for kc in range(8):
                                  for half in range(2):
                                      mm(PS.v(slice(2048 * st_ + 1024 * part + 512 * half, 2048 * st_ + 1024 * part + 512 * half + 512)),
                                         wb.v(kc, slice(128 * part, 128 * part + 128)), hT.v(kc, slice(512 * half, 512 * half + 512)), kc == 0, kc == 7)
                          yA, yG, sG = ya[i % 3], yg[st_], sg[i % 3]
                          for part, pX, yX in ((0, pA, yA), (1, pG, yG)):
                              ch = part * NFC + i
                              w0 = vcol(C_CW + (l * 3 + 0) * 44 + ch)
                              w1 = vcol(C_CW + (l * 3 + 1) * 44 + ch)
                              w2 = vcol(C_CW + (l * 3 + 2) * 44 + ch)
                              bb = vcol(C_CB + l * 44 + ch)
                              lo_ = pX_lo(st_, part)
                              act(yX.v(), pX, AF.Identity, bias=bb, scale=w2)
                              if sti == 0:
                                  cp(uh.v(ch), PS.v(slice(lo_ + 1022, lo_ + 1024)), eng="act")
                              act(yX.v(slice(1, 2)), PS.v(slice(lo_, lo_ + 1)), AF.Identity, bias=yX.v(slice(1, 2)), scale=w1)
                              if sti == 1:
                                  act(yX.v(slice(1, 2)), uh.v(ch, slice(1, 2)), AF.Identity, bias=yX.v(slice(1, 2)), scale=w0)
                                  act(yX.v(slice(0, 1)), uh.v(ch, slice(1, 2)), AF.Identity, bias=yX.v(slice(0, 1)), scale=w1)
                                  act(yX.v(slice(0, 1)), uh.v(ch, slice(0, 1)), AF.Identity, bias=yX.v(slice(0, 1)), scale=w0)
                              stt(yX.v(slice(2, 1024)), PS.v(slice(lo_ + 1, lo_ + 1023)), w1, yX.v(slice(2, 1024)), ALU.mult, ALU.add)
                              stt(yX.v(slice(2, 1024)), PS.v(slice(lo_, lo_ + 1022)), w0, yX.v(slice(2, 1024)), ALU.mult, ALU.add)
                          if prev_gate[0]:
                              prev_gate[0]()

                          def gate(i=i, yA=yA, yG=yG, sG=sG):
                              act(sG.v(), yG.v(), AF.Silu)
                              tt(actb.v(i), yA.v(), sG.v(), ALU.mult, eng="pool")
                          prev_gate[0] = gate
                      prev_gate[0]()
                      prev_gate[0] = None
                      for j in range(8):
                          wd = wdnb[j % 2]
                          for half in range(2):
                              pd = gps(512)
                              for fc in range(NFC):
                                  mm(pd, wd.v(fc), actb.v(fc, slice(512 * half, 512 * half + 512)), fc == 0, fc == NFC - 1)
                              xs = xT.v(j, slice(T0 + 512 * half, T0 + 512 * half + 512))
                              tt(xs, xs, pd, ALU.add)
                          if j + 2 < 8:
                              load_wdn(j + 2)
                      if between:
                          between(sti)

              def pX_lo(st_, part):
                  return 2048 * st_ + 1024 * part

              ffn(0)
              if s == 0:
                  dump("x2", xT.v())
              if stop == "ffn0":
                  finish_early()
                  break

              wi = {"i": 0}

              def wsm_load(src):
                  b = wsm[wi["i"] % 2]
                  S.dma("pool", b.v().ap, src, "wsm%d" % (wi["i"] % 2), writes=[b.v()])
                  wi["i"] += 1
                  return b

              PB = (0, 1, 2, 3)
              def proj_a(ti, sq_done=False, part="all"):
                  t0 = 512 * ti
                  rb = rstd[ti % 2]
                  if part in ("all", "norm"):
                      if not sq_done:
                          norm_sq(t0)
                      norm_fin(rb, ti % 2)
                  if part == "norm":
                      return
                  for c in range(8):
                      stt(hkv.v(c), xT.v(c, slice(t0, t0 + 512)), g32(C_KV + c), rb.v(), ALU.mult, ALU.mult)
                  for c in range(8):
                      stt(hq.v(c), xT.v(c, slice(t0, t0 + 512)), g32(C_NG + (1 * 2 + 0) * 8 + c), rb.v(), ALU.mult, ALU.mult)
                  if ti == 0:
                      chk("kv_a")
              def proj_b(ti, nxt=False, pre0=None, mid=None):
                  t0 = 512 * ti
                  if nxt:
                      norm_sq(t0 + 512)
                  def proj_chunks(wsrc, hsrc, dstf, maxf, pre=None):
                      pend = None
                      for j in range(8):
                          if j % 2 == 0:
                              wb = pre if (j == 0 and pre is not None) else wsm_load(wsrc[j // 2])
                          pk = gps(512, PB)
                          for kc in range(8):
                              mm(pk, wb.v(j % 2, kc), hsrc.v(kc), kc == 0, kc == 7)
                          dst = dstf(j)
                          cp(dst, pk, eng="dve")
                          nb = PT[0][j % 2].v()
                          act(nb, dst, AF.Square)
                          if pend:
                              pend()

                          def fin(j=j, nb=nb):
                              pn = gps(512, PB)
                              mm(pn, ones, nb, True, True)
                              maxf(j, pn)
                          pend = fin
                      return pend

                  def kmax(j, pn):
                      if ti == 0:
                          S.op("dve", lambda e, pn=pn, j=j: e.tensor_reduce(out=kq2.v(slice(j, j + 1)).ap, in_=pn.ap, axis=mybir.AxisListType.X, op=ALU.max),
                               reads=[pn], writes=[kq2.v(slice(j, j + 1))])
                      else:
                          S.op("dve", lambda e, pn=pn: e.tensor_reduce(out=kq2.v(slice(32, 33)).ap, in_=pn.ap, axis=mybir.AxisListType.X, op=ALU.max),
                               reads=[pn], writes=[kq2.v(slice(32, 33))])
                          tt(kq2.v(slice(j, j + 1)), kq2.v(slice(j, j + 1)), kq2.v(slice(32, 33)), ALU.max)

                  def qmax(j, pn):
                      S.op("dve", lambda e, pn=pn, j=j: e.tensor_reduce(out=kq2.v(slice(8 + j, 9 + j)).ap, in_=pn.ap, axis=mybir.AxisListType.X, op=ALU.max),
                           reads=[pn], writes=[kq2.v(slice(8 + j, 9 + j))])

                  lastk = proj_chunks(wk_d, hkv, lambda j: KT.v(j, slice(t0, t0 + 512)), kmax, pre=pre0)
                  if mid:
                      mid()
                  for qv in range(4):
                      wv_ = wvb[qv % 2]
                      S.dma("pool", wv_.v().ap, wv_d[qv], "wv%d" % (qv % 2), writes=[wv_.v()])
                      for tk in range(4):
                          pv = gps(256, PB)
                          for kc in range(8):
                              mm(pv, hkv.v(kc, slice(128 * tk, 128 * tk + 128)), wv_.v(kc), kc == 0, kc == 7)
                          evac_copy(Vb.v(4 * ti + tk, slice(256 * qv, 256 * qv + 256)), pv)
                  lastk()
                  lastq = proj_chunks(wq_d, hq, lambda j: QT.v(j), qmax)
                  lastq()
                  tt(kq2.v(slice(16, 24)), kq2.v(slice(0, 8)), kq2.v(slice(8, 16)), ALU.mult)
                  act(kq2.v(slice(24, 32)), kq2.v(slice(16, 24)), AF.Ln)
                  act(kq2.v(slice(16, 24)), kq2.v(slice(24, 32)), AF.Exp, scale=0.5)
                  S.op("dve", lambda e: e.tensor_scalar_mul(out=kq2.v(slice(16, 24)).ap, in0=kq2.v(slice(16, 24)).ap, scalar1=-SM_SCALE),
                       reads=[kq2.v(slice(16, 24))], writes=[kq2.v(slice(16, 24))])
                  if nxt:
                      proj_a(ti + 1, sq_done=True, part="norm")
                  if s == 0 and ti == 1:
                      dump("kq2", kq2.v())
                  if ti == 0:
                      chk("kvq")

              def heads(ti):
                  nk = 4 * ti + 4
                  pendA, pendB = [], []

                  def flush_pending(pending):
                      while pending:
                          pending.pop(0)()
                  Sb = lambda c, b: 512 * (2 * c + b)
                  Ob = lambda c: 512 * (4 + c)
                  Lb = lambda c: 512 * (6 + c)
                  for head in range(8):
                      def s_mm(kt, head=head):
                          i = kt - 4 * ti
                          lo = 128 * i if i >= 0 else 0
                          for c in range(2):
                              pr = (64 * c, 64 * c + 64)
                              base = Sb(c, kt % 2)
                              mm(PS.v(slice(base + lo, base + 512)), KT.v(head, slice(128 * kt, 128 * kt + 128), p=pr), QT.v(head, slice(lo, 512), p=pr), True, i < 0)
                          if i >= 0:
                              for c in range(2):
                                  base = Sb(c, kt % 2)
                                  mm(PS.v(slice(base + lo, base + lo + 128)), ident, negmask, False, True)
                          return lo

                      def exps(kt, lo, head, cs=(0, 1)):
                          for c in cs:
                              base = Sb(c, kt % 2)
                              act(PT[c][kt % 2].v(slice(lo, 512)), PS.v(slice(base + lo, base + 512)), AF.Exp, bias=kq2.v(slice(16 + head, 17 + head)), scale=SM_SCALE)

                      los = {}
                      if head == 0:
                          los[0] = s_mm(0)
                      else:
                          los[0] = 0
                      for kt in range(nk):
                          if kt + 1 < nk:
                              los[kt + 1] = s_mm(kt + 1)
                          elif head + 1 < 8:
                              s_mm(0, head + 1)
                          lo = los[kt]
                          if not (kt == 0 and head > 0):
                              exps(kt, lo, head)
                          if kt == 2:
                              flush_pending(pendA)
                          if kt == 3:
                              flush_pending(pendB)
                          for c in range(2):
                              pt = PT[c][kt % 2].v(slice(lo, 512))
                              mm(PS.v(slice(Ob(c) + lo, Ob(c) + 512)), Vb.v(kt, slice(128 * head, 128 * head + 128)), pt, kt == 0, kt == nk - 1)
                              mm(PS.v(slice(Lb(c) + lo, Lb(c) + 512)), ones, pt, kt == 0, kt == nk - 1)
                      O0, O1 = PS.v(slice(Ob(0), Ob(0) + 512)), PS.v(slice(Ob(1), Ob(1) + 512))
                      L0, L1 = PS.v(slice(Lb(0), Lb(0) + 512)), PS.v(slice(Lb(1), Lb(1) + 512))
                      if head + 1 < 8:
                          exps(0, 0, head + 1, cs=(0,))
                      act(ft[0].v(), L0, AF.Ln)
                      cp(ft[2].v(), O0, eng="dve")
                      if head + 1 < 8:
                          exps(0, 0, head + 1, cs=(1,))
                      act(ft[1].v(), L1, AF.Ln)
                      cp(ft[3].v(), O1, eng="dve")

                      def finA(head=head):
                          act(ft[0].v(), ft[0].v(), AF.Exp, scale=-1.0)
                          act(ft[1].v(), ft[1].v(), AF.Exp, scale=-1.0)
                          tt(ft[2].v(), ft[2].v(), ft[0].v(), ALU.mult)
                          tt(ft[3].v(), ft[3].v(), ft[1].v(), ALU.mult)
                          stt(oT.v(head), ft[3].v(), neglam, ft[2].v(), ALU.mult, ALU.add)
                      pendA.append(finA)
                      pendB.append(lambda head=head: tt(osq(head), oT.v(head), oT.v(head), ALU.mult, eng="pool"))
                      if ti == 0 and head == 0:
                          chk("att1")
                      if ti == 0 and head == 1:
                          chk("att2")
                  flush_pending(pendA)
                  flush_pending(pendB)
              def osq(head):
                  return QT.v(head)

              p2bank = {}

              def part2_pe(hs):
                  for head in hs:
                      pn = PS.v(slice(512 * ((head + 4) % 8), 512 * ((head + 4) % 8) + 512))
                      p2bank[head] = pn
                      mm(pn, ones, osq(head), True, True)

              def part2_rest(hs):
                  for head in hs:
                      pn = p2bank[head]
                      fa, fb = ft[(2 * head) % 4], ft[(2 * head + 1) % 4]
                      act(fa.v(), pn, AF.Ln, bias=eps_h)
                      act(fb.v(), fa.v(), AF.Exp, scale=-0.5)
                      stt(oT.v(head), oT.v(head), gsub, fb.v(), ALU.mult, ALU.mult)
              def wo_proj(ti):
                  t0 = 512 * ti
                  for j in range(8):
                      if j % 2 == 0:
                          wb = wsm_load(wo_d[j // 2])
                      pw_ = gps(512, PB)
                      for hh in range(8):
                          mm(pw_, wb.v(j % 2, hh), oT.v(hh), hh == 0, hh == 7)
                      xs = xT.v(j, slice(t0, t0 + 512))
                      tt(xs, xs, pw_, ALU.add)
              proj_a(0)
              proj_b(0, nxt=True)
              proj_a(1, sq_done=True, part="stt")
              for ti in range(4):
                  heads(ti)
                  if ti < 3:
                      pre0 = wsm_load(wk_d[0])
                      part2_pe(range(0, 4))
                      part2_rest(range(0, 4))
                      part2_pe(range(4, 7))
                      part2_rest(range(4, 7))

                      def mid():
                          part2_pe(range(7, 8))
                          part2_rest(range(7, 8))
                      proj_b(ti + 1, nxt=(ti + 1 < 3), pre0=pre0, mid=mid)
                  else:
                      part2_pe(range(0, 4))
                      part2_rest(range(0, 4))
                      part2_pe(range(4, 8))
                      part2_rest(range(4, 8))
                  wo_proj(ti)
                  if ti + 2 < 4:
                      proj_a(ti + 2, sq_done=True, part="stt")
              if s == 0:
                  dump("x3", xT.v())
              if stop == "att":
                  finish_early()
                  break

              def between(sti, s=s):
                  for ti in (2 * sti, 2 * sti + 1):
                      final_tile(s, ti)
                  if s + 1 < NSEQ:
                      p1_weights()
                      for ti in (2 * sti, 2 * sti + 1):
                          p1_tile_pe(ti)
              ffn(1, between)

        try:
            _main()
        except _Stop:
            pass
        S.emit_all(final_wait_ops=out_dmas)
        print("ops", {e: len(S.ops[e]) for e in ENGINES}, "sems", S.nsems, "waits", S.nwaits)
    return nc


_PROG = {}


def _host_layouts(inp):
    f = lambda a: np.ascontiguousarray(np.asarray(a, dtype=np.float32))
    w_up = f(inp["w_up"])
    a = w_up[:, :, :DFF].reshape(2, 8, 128, NFC, 128)
    g = w_up[:, :, DFF:].reshape(2, 8, 128, NFC, 128)
    wup = np.stack([a, g], axis=4)
    wup = f(wup.transpose(0, 3, 2, 1, 4, 5).reshape(2, NFC, 128, 8, 256))
    wdn = f(f(inp["w_down"]).reshape(2, NFC, 128, 8, 128).transpose(0, 3, 2, 1, 4))
    pw = f(f(inp["pool_w"])[0].reshape(4, 2, 128, 256).transpose(2, 0, 1, 3))
    w_kv = f(inp["w_kv"])
    hperm = lambda w: w.reshape(1024, 2, 8, 64).transpose(0, 2, 1, 3).reshape(1024, 1024)
    wk = f(hperm(w_kv[:, :1024]).reshape(8, 128, 8, 128).transpose(2, 1, 0, 3))
    wv = f(w_kv[:, 1024:].reshape(8, 128, 4, 256).transpose(2, 1, 0, 3))
    wq = f(hperm(f(inp["wq"])[0]).reshape(8, 128, 8, 128).transpose(2, 1, 0, 3))
    wo = f(f(inp["wo"])[0].reshape(8, 128, 8, 128).transpose(2, 1, 0, 3))
    pair = lambda w: f(w.reshape(4, 2, 128, 8, 128).transpose(0, 2, 1, 3, 4))
    wk, wq, wo = pair(wk), pair(wq), pair(wo)
    vec = np.zeros((128, NV), np.float32)
    pm = lambda v: f(v).reshape(-1, 128).T
    vec[:, C_NG:C_NG + 32] = pm(f(inp["norm_g"]).reshape(-1))
    vec[:, C_KV:C_KV + 8] = pm(inp["kv_norm"])
    vec[:, C_FIN:C_FIN + 8] = pm(inp["final_norm"])
    vec[:, C_PS:C_PS + 8] = pm(f(inp["pool_scale"])[0])
    vec[:, C_SUB] = f(inp["subln_g"])[0]
    vec[:, C_CW:C_CW + 264] = pm(f(inp["conv_w"]).reshape(-1))
    vec[:, C_CB:C_CB + 88] = pm(f(inp["conv_b"]).reshape(-1))
    vec[:, C_LAM:C_LAM + 256] = f(inp["lam"])[0].reshape(1, 256)
    inv = (1.0 / np.arange(1, 17, dtype=np.float64)).astype(np.float32)
    vec[:, C_INV:C_INV + 16] = inv
    vec[:, C_INV + 16:C_INV + 32] = inv
    cst = np.zeros((128, 11, 128), np.float32)
    for gi, w in enumerate((2, 4, 8, 16)):
        cst[:, 3 + 2 * gi, :] = np.eye(128, dtype=np.float32) * (1.0 / w - 1.0)
        cst[:, 4 + 2 * gi, :] = np.eye(128, dtype=np.float32) * (1.0 / w)
    cst[:, 0, :] = np.eye(128, dtype=np.float32)
    cst[:, 1, :] = np.where(np.arange(128)[:, None] > np.arange(128)[None, :], NEG, 0.0)
    cst[:, 2, :] = 1.0
    return dict(vec=vec, cst=cst, wup=wup, wdn=wdn, pw=pw, wk=wk, wv=wv, wq=wq, wo=wo)


def kernel(**inputs):
    x = np.asarray(inputs["x"], dtype=np.float32)
    shared = _host_layouts(inputs)
    if "nc" not in _PROG:
        _PROG["nc"] = build_program()
    nc = _PROG["nc"]
    in_maps = []
    for c in range(8):
        xs = x[NSEQ * c:NSEQ * c + NSEQ]
        xT = np.ascontiguousarray(xs.transpose(0, 2, 1).reshape(NSEQ, 8, 128, SEQ))
        m = dict(shared)
        m["xT"] = xT
        in_maps.append(m)
    res = run_bass_kernel_spmd(nc, in_maps, core_ids=list(range(8)))
    outs = []
    for c in range(8):
        oT = np.asarray(res.results[c]["outT"]).reshape(NSEQ, D, SEQ)
        outs.append(oT.transpose(0, 2, 1))
    return np.ascontiguousarray(np.concatenate(outs, axis=0).astype(np.float32))
```

```python
from contextlib import ExitStack

import concourse.bass as bass
import concourse.mybir as mybir

F32 = mybir.dt.float32
BF16 = mybir.dt.bfloat16
ALU = mybir.AluOpType
AF = mybir.ActivationFunctionType

ENGINES = ("pe", "act", "dve", "pool", "sp")
TRACE_OPS = None


class Buf:
    def __init__(self, S, name, shape, dtype, space="sbuf", arena=None, off=0):
        self.S = S
        self.name = name
        self.shape = tuple(shape)
        self.dtype = dtype
        self.esz = 2 if dtype == F32 else 1
        self.strides = []
        st = 1
        for d in reversed(self.shape):
            self.strides.insert(0, st)
            st *= d
        self.size = st
        self.recs = []
        self.alias = []
        self.is_psum = (space == "psum")
        if arena is None:
            full = [128] + list(shape)
            if space == "sbuf":
                self.h = S.stack.enter_context(S.nc.sbuf_tensor(name, full, dtype))
            else:
                self.h = S.stack.enter_context(S.nc.psum_tensor(name, full, dtype))
            self.base = 0
            self.root = self.h
            self.carved = []
        else:
            assert off % 4 == 0
            nbytes = self.size * self.esz * 2
            assert nbytes % 4 == 0 and off + nbytes <= arena.size * 4, (name, off, nbytes)
            w0 = off // 4
            ap = arena.h[:, w0:w0 + nbytes // 4]
            if dtype != F32:
                ap = ap.bitcast(dtype)
            if len(self.shape) > 1:
                names = " ".join("d%d" % i for i in range(len(self.shape)))
                kw = {"d%d" % i: self.shape[i] for i in range(len(self.shape) - 1)}
                ap = ap.rearrange("p (%s) -> p %s" % (names, names), **kw)
            self.root = ap
            self.base = off // 2
            self.end = self.base + self.size * self.esz
            for ob in arena.carved:
                if ob.base < self.end and self.base < ob.end:
                    ob.alias.append(self)
                    self.alias.append(ob)
            arena.carved.append(self)

    def v(self, *idx, p=None):
        idx = list(idx) + [slice(None)] * (len(self.shape) - len(idx))
        norm = []
        for d, i in zip(self.shape, idx):
            if isinstance(i, int):
                norm.append((i, i + 1, True))
            else:
                a = 0 if i.start is None else i.start
                b = d if i.stop is None else i.stop
                assert 0 <= a < b <= d, (self.name, idx)
                norm.append((a, b, False))
        n = len(norm)
        j = n - 1
        while j > 0 and norm[j][0] == 0 and norm[j][1] == self.shape[j]:
            j -= 1
        offs = [0]
        for d in range(j):
            a, b, _ = norm[d]
            offs = [o + i * self.strides[d] for o in offs for i in range(a, b)]
        a, b, _ = norm[j]
        e = self.esz
        ranges = [(self.base + (o + a * self.strides[j]) * e, self.base + (o + b * self.strides[j]) * e) for o in offs]
        if self.is_psum:
            BK = 1024
            ranges = sorted(set((lo // BK * BK, (hi + BK - 1) // BK * BK) for (lo, hi) in ranges))
        sl = [slice(None) if p is None else slice(p[0], p[1])]
        for (a, b, isint) in norm:
            sl.append(a if isint else slice(a, b))
        ap = self.root[tuple(sl)]
        return View(self, ap, ranges)


class View:
    def __init__(self, buf, ap, ranges):
        self.buf = buf
        self.ap = ap
        self.ranges = ranges


class Op:
    __slots__ = ("eng", "emit", "deps", "signal", "val", "semkey", "inc", "idx", "name")

    def __init__(self, eng, emit, name=""):
        self.eng = eng
        self.emit = emit
        self.deps = {}
        self.signal = False
        self.val = None
        self.semkey = None
        self.inc = 1
        self.name = name


class Sched:
    def __init__(self, nc, stack):
        self.nc = nc
        self.stack = stack
        self.ops = {e: [] for e in ENGINES}
        self.dma_groups = {}
        self.all_ops = []

    def sbuf(self, name, shape, dtype):
        return Buf(self, name, shape, dtype, "sbuf")

    def psum(self, name, shape, dtype=F32):
        return Buf(self, name, shape, dtype, "psum")

    def carve(self, arena, name, shape, dtype, off):
        return Buf(self, name, shape, dtype, "sbuf", arena=arena, off=off)

    def _track(self, op, reads, writes):
        for vw in reads:
            for b in [vw.buf] + vw.buf.alias:
                if not b.recs:
                    continue
                for (lo, hi) in vw.ranges:
                    for (rlo, rhi, rop, isw) in b.recs:
                        if rlo < hi and lo < rhi:
                            if isw:
                                op.deps[rop] = True
                            elif b.is_psum and rop.eng != op.eng:
                                op.deps.setdefault(rop, False)
        for vw in writes:
            for b in [vw.buf] + vw.buf.alias:
                if not b.recs:
                    continue
                for (lo, hi) in vw.ranges:
                    for (rlo, rhi, rop, isw) in b.recs:
                        if rlo < hi and lo < rhi and rop is not op:
                            op.deps.setdefault(rop, False)
        for vw in writes:
            for b in [vw.buf] + vw.buf.alias:
                for (lo, hi) in vw.ranges:
                    if b.recs:
                        b.recs = [r for r in b.recs if not (r[0] >= lo and r[1] <= hi)]
            for (lo, hi) in vw.ranges:
                vw.buf.recs.append((lo, hi, op, True))
        for vw in reads:
            b = vw.buf
            for (lo, hi) in vw.ranges:
                if op.semkey is None:
                    b.recs = [r for r in b.recs if not (r[0] == lo and r[1] == hi and (not r[3]) and r[2].eng == op.eng and r[2].semkey is None)]
                b.recs.append((lo, hi, op, False))

    def op(self, eng, emit, reads=(), writes=(), name=""):
        o = Op(eng, emit, name)
        self._track(o, reads, writes)
        o.deps.pop(o, None)
        self.ops[eng].append(o)
        self.all_ops.append(o)
        return o

    def dma(self, eng, out_ap, in_ap, group, reads=(), writes=(), name=""):
        def emit(e, out_ap=out_ap, in_ap=in_ap):
            return e.dma_start(out=out_ap, in_=in_ap)
        o = Op(eng, emit, name)
        o.semkey = ("dma", group)
        n = self.dma_groups.get(group, 0) + 1
        self.dma_groups[group] = n
        o.val = 16 * n
        o.inc = 16
        o.signal = True
        self._track(o, reads, writes)
        o.deps.pop(o, None)
        self.ops[eng].append(o)
        self.all_ops.append(o)
        return o

    def emit_all(self, final_wait_ops=()):
        nc = self.nc
        for o in self.all_ops:
            for d in o.deps:
                if d.semkey is None:
                    if d.eng == o.eng and o.semkey is None and d.eng == "pe":
                        continue
                    d.signal = True
        for o in final_wait_ops:
            if o.semkey is None:
                o.signal = True
        counts = {}
        for e in ENGINES:
            c = 0
            for o in self.ops[e]:
                if o.semkey is None and o.signal:
                    c += 1
                    o.val = c
            counts[e] = c
        sems = {}
        for e in ENGINES:
            if counts[e] > 0:
                sems[("eng", e)] = self.stack.enter_context(nc.semaphore("s_" + e))
        for g in self.dma_groups:
            sems[("dma", g)] = self.stack.enter_context(nc.semaphore("d_" + str(g)))
        self.nsems = len(sems)

        def key_of(d):
            return d.semkey if d.semkey is not None else ("eng", d.eng)

        eng_obj = {"pe": "tensor", "act": "scalar", "dve": "vector", "pool": "gpsimd", "sp": "sync"}
        nwaits = {e: 0 for e in ENGINES}

        def body(ename, extra_final=None):
            def fn(eng):
                seen = {}
                for o in self.ops[ename]:
                    need = {}
                    for d in o.deps:
                        if d.semkey is None and d.eng == ename and o.semkey is None and ename == "pe":
                            continue
                        k = key_of(d)
                        if seen.get(k, 0) >= d.val:
                            continue
                        if need.get(k, 0) < d.val:
                            need[k] = d.val
                    if TRACE_OPS is not None:
                        TRACE_OPS.append((ename, len(TRACE_OPS), o.emit.__code__.co_firstlineno, dict(need), (key_of(o), o.val) if o.signal else None))
                    for k, v in need.items():
                        eng.wait_ge(sems[k], v)
                        seen[k] = v
                        nwaits[ename] += 1
                    ins = o.emit(eng)
                    if o.signal:
                        ins.then_inc(sems[key_of(o)], o.inc)
                if extra_final:
                    for d in extra_final:
                        k = key_of(d)
                        if seen.get(k, 0) < d.val:
                            eng.wait_ge(sems[k], d.val)
                            seen[k] = d.val
            return fn

        with nc.Block() as block:
            for e in ENGINES:
                if not self.ops[e] and not (e == "sp" and final_wait_ops):
                    continue
                dec = getattr(block, eng_obj[e])
                dec(body(e, final_wait_ops if e == "sp" else None))
        self.nwaits = nwaits

import math, os
KDBG = os.environ.get('KDBG', '')
import numpy as np
from concourse.bass_utils import run_bass_kernel_spmd

D = 1024
SEQ = 2048
NSEQ = 2
DFF = 2816
NFC = 22
EPS = 1e-6
LAM_INIT = 0.8 - 0.6 * math.exp(-0.3 * 1)
SM_SCALE = 0.125
NEG = -30000.0

C_NG, C_KV, C_FIN, C_PS, C_SUB, C_CW, C_CB, C_LAM, C_INV, NV = 0, 32, 40, 48, 56, 57, 321, 409, 665, 697


def build_program(dbg=None, stop=None):
    nc = bass.Bass("TRN2", target_bir_lowering=False)
    dt_in = lambda n, s: nc.dram_tensor(n, s, F32, kind="ExternalInput").ap()
    xT_d = dt_in("xT", [NSEQ, 8, 128, SEQ])
    vec_d = dt_in("vec", [128, NV])
    cst_d = dt_in("cst", [128, 11, 128])
    wup_d = dt_in("wup", [2, NFC, 128, 8, 256])
    wdn_d = dt_in("wdn", [2, 8, 128, NFC, 128])
    pw_d = dt_in("pw", [128, 4, 2, 256])
    wk_d = dt_in("wk", [4, 128, 2, 8, 128])
    wv_d = dt_in("wv", [4, 128, 8, 256])
    wq_d = dt_in("wq", [4, 128, 2, 8, 128])
    wo_d = dt_in("wo", [4, 128, 2, 8, 128])
    out_d = nc.dram_tensor("outT", [NSEQ, 8, 128, SEQ], F32, kind="ExternalOutput").ap()
    dbg_d = {}
    if dbg:
        for k, shp in dbg.items():
            dbg_d[k] = nc.dram_tensor("dbg_" + k, shp, F32, kind="ExternalOutput").ap()

    with ExitStack() as st:
        S = Sched(nc, st)
        xT = S.sbuf("xTs", [8, SEQ], F32)
        vec = S.sbuf("vecs", [NV], F32)
        vec2 = S.sbuf("vec2", [64], F32)
        cst = S.sbuf("csts", [11, 128], BF16)
        sq = S.sbuf("sqs", [8, 512], BF16)
        rstd = [S.sbuf("rstd%d" % i, [512], F32) for i in range(2)]
        uh = S.sbuf("uh", [2 * NFC, 2], F32)
        hhalo = S.sbuf("hhalo", [8, 16], F32)
        kq2 = S.sbuf("kq2", [40], F32)

        ARENA_BYTES = 124 * 1024
        arena = S.sbuf("arena", [ARENA_BYTES // 4], F32)
        PS = S.psum("PS", [4096], F32)

        class Bump:
            def __init__(self):
                self.off = 0

            def take(self, name, shape, dtype):
                n = 1
                for d in shape:
                    n *= d
                nb = n * (4 if dtype == F32 else 2)
                nb = (nb + 31) // 32 * 32
                b = S.carve(arena, name, shape, dtype, self.off)
                self.off += nb
                assert self.off <= ARENA_BYTES, (name, self.off)
                return b

        bp = Bump()
        hb = [bp.take("hb%d" % i, [8, 528], F32) for i in range(2)]
        tA = bp.take("tA", [2, 528], F32)
        tB = bp.take("tB", [2, 528], F32)
        tC = bp.take("tC", [2, 528], F32)
        tD = bp.take("tD", [2, 528], F32)
        dT = [bp.take("dT%d" % i, [8, 512], BF16) for i in range(2)]
        hb16 = [bp.take("hb16_%d" % i, [8, 528], BF16) for i in range(2)]
        hfx = bp.take("hfx", [8, 32], F32)
        pwb = bp.take("pwb", [4, 2, 256], BF16)
        bf_ = Bump()
        hT = bf_.take("hT", [8, 1024], BF16)
        actb = bf_.take("actb", [NFC, 1024], BF16)
        ya = [bf_.take("ya%d" % i, [1024], F32) for i in range(3)]
        yg = [bf_.take("yg%d" % i, [1024], F32) for i in range(2)]
        sg = [bf_.take("sg%d" % i, [1024], F32) for i in range(3)]
        NWUP = 5
        wupb = [bf_.take("wupb%d" % i, [8, 256], BF16) for i in range(NWUP)]
        wdnb = [bf_.take("wdnb%d" % i, [NFC, 128], BF16) for i in range(2)]
        ba = Bump()
        KT = ba.take("KT", [8, SEQ], BF16)
        Vb = ba.take("Vb", [16, 1024], BF16)
        QT = ba.take("QT", [8, 512], BF16)
        hkv = ba.take("hkv", [8, 512], BF16)
        hq = ba.take("hq", [8, 512], BF16)
        PT = [[ba.take("PT%d%d" % (c, i), [512], BF16) for i in range(2)] for c in range(2)]
        ft = [ba.take("ft%d" % i, [512], F32) for i in range(4)]
        oT = ba.take("oT", [8, 512], BF16)
        NWS = 4
        wsm = [ba.take("wsm%d" % i, [2, 8, 128], BF16) for i in range(NWS)]
        wsmv = [S.carve(arena, "wsmv%d" % i, [8, 256], BF16, wsm[i].base * 2) for i in range(NWS)]

        class _NS:
            def v(self, *a):
                return sq.v(7)
        nsq = _NS()
        bo = Bump()
        ost = [bo.take("ost%d" % i, [8, 512], F32) for i in range(1)]

        def apof(x):
            return x.ap if isinstance(x, View) else x

        def rd(*xs):
            return [x for x in xs if isinstance(x, View)]

        def mm(out, lhsT, rhs, start, stop):
            S.op("pe", lambda e: e.matmul(out.ap, lhsT=lhsT.ap, rhs=rhs.ap, start=start, stop=stop),
                 reads=[lhsT, rhs], writes=[out])

        def act(out, in_, func, bias=0.0, scale=1.0):
            if isinstance(bias, View) or isinstance(scale, View):
                f = lambda e: e.activation(out=out.ap, in_=in_.ap, func=func, bias=apof(bias), scale=apof(scale))
            else:
                f = lambda e: e.activation(out=out.ap, in_=in_.ap, func=func, bias=bias, scale=scale)
            return S.op("act", f, reads=[in_] + rd(bias, scale), writes=[out])

        def stt(out, in0, scalar, in1, op0, op1, eng="dve"):
            return S.op(eng, lambda e: e.scalar_tensor_tensor(out=out.ap, in0=in0.ap, scalar=apof(scalar), in1=in1.ap, op0=op0, op1=op1),
                        reads=[in0, in1] + rd(scalar), writes=[out])

        def tt(out, in0, in1, op, eng="dve"):
            return S.op(eng, lambda e: e.tensor_tensor(out=out.ap, in0=in0.ap, in1=in1.ap, op=op), reads=[in0, in1], writes=[out])

        def ts(out, in0, s1, s2, op0, op1, eng="dve"):
            return S.op(eng, lambda e: e.tensor_scalar(out=out.ap, in0=in0.ap, scalar1=apof(s1), scalar2=apof(s2), op0=op0, op1=op1),
                        reads=[in0] + rd(s1, s2), writes=[out])

        def cp(out, in_, eng="dve"):
            if eng == "act":
                return S.op("act", lambda e: e.copy(out=out.ap, in_=in_.ap), reads=[in_], writes=[out])
            return S.op(eng, lambda e: e.tensor_copy(out=out.ap, in_=in_.ap), reads=[in_], writes=[out])

        def memset(out, val, eng="dve"):
            return S.op(eng, lambda e: e.memset(out.ap, val), writes=[out])

        def vcol(c):
            return vec.v(slice(c, c + 1))

        def v2col(c):
            return vec2.v(slice(c, c + 1))

        ident = cst.v(0)
        negmask = cst.v(1)
        ones = cst.v(2)

        def dump(name, view):
            if name in dbg_d:
                S.dma("sp", dbg_d[name], view.ap, "dbg_" + name, reads=[view])

        rot = {"i": 0}

        def gps(n=512, banks=(0, 1, 2, 3, 4, 5, 6, 7)):
            b = banks[rot["i"] % len(banks)]
            rot["i"] += 1
            return PS.v(slice(512 * b, 512 * b + n))

        evq = {"i": 0}

        def evac_copy(out, in_):
            evq["i"] += 1
            cp(out, in_, eng="act" if evq["i"] % 2 else "dve")

        S.dma("sp", vec.v().ap, vec_d, "vec", writes=[vec.v()])
        S.dma("pool", cst.v().ap, cst_d, "cst", writes=[cst.v()])
        invc = S.sbuf("invc", [2, 16], F32)
        S.dma("sp", invc.v().ap, vec_d[:, C_INV:C_INV + 32].rearrange("p (a b) -> p a b", a=2), "invc", writes=[invc.v()])
        S.op("dve", lambda e: e.tensor_scalar_mul(out=vec2.v(slice(0, 48)).ap, in0=vec.v(slice(0, 48)).ap, scalar1=32.0),
             reads=[vec.v(slice(0, 48))], writes=[vec2.v(slice(0, 48))])
        S.op("dve", lambda e: e.tensor_scalar_mul(out=v2col(48).ap, in0=vcol(C_SUB).ap, scalar1=(1.0 - LAM_INIT) * math.sqrt(128.0)),
             reads=[vcol(C_SUB)], writes=[v2col(48)])
        l01 = rstd[0]
        tt(l01.v(slice(0, 64)), vec.v(slice(C_LAM, C_LAM + 64)), vec.v(slice(C_LAM + 64, C_LAM + 128)), ALU.mult)
        tt(l01.v(slice(64, 128)), vec.v(slice(C_LAM + 128, C_LAM + 192)), vec.v(slice(C_LAM + 192, C_LAM + 256)), ALU.mult)
        S.op("dve", lambda e: e.tensor_reduce(out=v2col(50).ap, in_=l01.v(slice(0, 64)).ap, axis=mybir.AxisListType.X, op=ALU.add),
             reads=[l01.v(slice(0, 64))], writes=[v2col(50)])
        S.op("dve", lambda e: e.tensor_reduce(out=v2col(51).ap, in_=l01.v(slice(64, 128)).ap, axis=mybir.AxisListType.X, op=ALU.add),
             reads=[l01.v(slice(64, 128))], writes=[v2col(51)])
        act(vec2.v(slice(52, 54)), vec2.v(slice(50, 52)), AF.Exp)
        tt(v2col(54), v2col(53), v2col(52), ALU.subtract)
        S.op("dve", lambda e: e.tensor_scalar_add(out=v2col(49).ap, in0=v2col(54).ap, scalar1=-LAM_INIT), reads=[v2col(54)], writes=[v2col(49)])
        memset(v2col(55), D * EPS)
        memset(v2col(56), 128.0 * EPS)
        eps_d = v2col(55)
        eps_h = v2col(56)
        neglam = v2col(49)
        gsub = v2col(48)

        def g32(idx):
            return v2col(idx)

        def norm_sq(t0):
            for c in range(8):
                act(sq.v(c), xT.v(c, slice(t0, t0 + 512)), AF.Square)

        def norm_fin(rbuf, psbank):
            pn = PS.v(slice(512 * psbank, 512 * psbank + 512))
            for c in range(8):
                mm(pn, ones, sq.v(c), c == 0, c == 7)
            act(rbuf.v(), pn, AF.Ln, bias=eps_d)
            act(rbuf.v(), rbuf.v(), AF.Exp, scale=-0.5)

        def norm_rstd(t0, rbuf, psbank):
            norm_sq(t0)
            norm_fin(rbuf, psbank)

        out_dmas = []

        def finish_early():
            od = [S.dma("sp", out_d[0, c], xT.v(c).ap, "out", reads=[xT.v(c)]) for c in range(8)]
            for o in od:
                o.val = od[-1].val
            out_dmas.extend(od)

        class _Stop(Exception):
            pass

        def chk(name):
            if stop == name:
                finish_early()
                raise _Stop()

        def load_x(s_, tiles=(0, 1, 2, 3)):
            for ti_ in tiles:
                xl = [S.dma("sp", xT.v(c, slice(512 * ti_, 512 * ti_ + 512)).ap, xT_d[s_, c, :, 512 * ti_:512 * ti_ + 512], "xin%d" % ti_,
                            writes=[xT.v(c, slice(512 * ti_, 512 * ti_ + 512))]) for c in range(8)]
                for o in xl:
                    o.val = xl[-1].val

        def p1_weights():
            S.dma("pool", pwb.v().ap, pw_d, "pw", writes=[pwb.v()])

        def p1_tile_pe(ti):
            t0 = 512 * ti
            hbc = hb16[ti % 2]
            if ti == 0:
                memset(hbc.v(slice(0, 8), slice(0, 16)), 0.0)
            else:
                cp(hbc.v(slice(0, 8), slice(0, 16)), hhalo.v(), eng="act")
            rb = rstd[ti % 2]
            norm_rstd(t0, rb, ti % 2)
            for c in range(8):
                stt(hbc.v(c, slice(16, 528)), xT.v(c, slice(t0, t0 + 512)), g32(C_NG + c), rb.v(), ALU.mult, ALU.mult)
            cp(hhalo.v(), hbc.v(slice(0, 8), slice(512, 528)), eng="act")
            dTc = dT[ti % 2]
            for gi, w in enumerate((2, 4, 8, 16)):
                for cc in (2 * gi, 2 * gi + 1):
                    pd = gps(512, banks=(2, 3, 4, 5, 6, 7))
                    for j in range(w):
                        mm(pd, cst.v(3 + 2 * gi + (1 if j else 0)), hbc.v(cc, slice(16 - j, 528 - j)), j == 0, j == w - 1)
                    cp(dTc.v(cc), pd, eng="act")
            if ti == 0:
                memset(hfx.v(slice(0, 8), slice(0, 16)), 0.0)
                for c in range(8):
                    stt(hfx.v(c, slice(16, 32)), xT.v(c, slice(0, 16)), g32(C_NG + c), rb.v(slice(0, 16)), ALU.mult, ALU.mult)
                for g, w in enumerate((2, 4, 8, 16)):
                    H = lambda lo, hi: hfx.v(slice(2 * g, 2 * g + 2), slice(lo, hi))
                    cur, other = tA, tB
                    tt(cur.v(slice(0, 2), slice(2, 32)), H(2, 32), H(1, 31), ALU.add)
                    sh = 2
                    while sh < w:
                        lo = 2 * sh
                        tt(other.v(slice(0, 2), slice(lo, 32)), cur.v(slice(0, 2), slice(lo, 32)), cur.v(slice(0, 2), slice(lo - sh, 32 - sh)), ALU.add)
                        cur, other = other, cur
                        sh *= 2
                    n = w - 1
                    tmp = other.v(slice(0, 2), slice(0, n))
                    tt(tmp, cur.v(slice(0, 2), slice(16, 16 + n)), invc.v(slice(0, 2), slice(0, n)), ALU.mult)
                    tt(dTc.v(slice(2 * g, 2 * g + 2), slice(0, n)), tmp, H(16, 16 + n), ALU.subtract)
            for oc in range(8):
                g, eh = oc // 2, oc % 2
                pp = gps(512, banks=(2, 3, 4, 5, 6, 7))
                for kc in range(2):
                    mm(pp, pwb.v(g, kc, slice(eh * 128, eh * 128 + 128)), dTc.v(2 * g + kc), kc == 0, kc == 1)
                stt(xT.v(oc, slice(t0, t0 + 512)), pp, vcol(C_PS + oc), xT.v(oc, slice(t0, t0 + 512)), ALU.mult, ALU.add)

        def p1_tile(ti):
            t0 = 512 * ti
            hbc = hb[ti % 2]
            if ti == 0:
                memset(hbc.v(slice(0, 8), slice(0, 16)), 0.0)
            else:
                cp(hbc.v(slice(0, 8), slice(0, 16)), hhalo.v(), eng="act")
            rb = rstd[ti % 2]
            norm_rstd(t0, rb, ti % 2)
            for c in range(8):
                stt(hbc.v(c, slice(16, 528)), xT.v(c, slice(t0, t0 + 512)), g32(C_NG + c), rb.v(), ALU.mult, ALU.mult)
            cp(hhalo.v(), hbc.v(slice(0, 8), slice(512, 528)), eng="act")
            dTc = dT[ti % 2]
            for g, w in ((3, 16), (0, 2), (2, 8), (1, 4)):
                pe_ = "dve"
                H = lambda lo, hi: hbc.v(slice(2 * g, 2 * g + 2), slice(lo, hi))
                cur, other = (tC, tD) if g >= 2 else (tA, tB)
                tt(cur.v(slice(0, 2), slice(2, 528)), H(2, 528), H(1, 527), ALU.add, eng=pe_)
                sh = 2
                while sh < w:
                    lo = 2 * sh
                    tt(other.v(slice(0, 2), slice(lo, 528)), cur.v(slice(0, 2), slice(lo, 528)), cur.v(slice(0, 2), slice(lo - sh, 528 - sh)), ALU.add, eng=pe_)
                    cur, other = other, cur
                    sh *= 2
                if pe_ == "dve":
                    stt(dTc.v(slice(2 * g, 2 * g + 2)), cur.v(slice(0, 2), slice(16, 528)), 1.0 / w, H(16, 528), ALU.mult, ALU.subtract)
                else:
                    tsm = other.v(slice(0, 2), slice(16, 528))
                    S.op("pool", lambda e, tsm=tsm, cur=cur, w=w: e.tensor_scalar_mul(out=tsm.ap, in0=cur.v(slice(0, 2), slice(16, 528)).ap, scalar1=1.0 / w),
                         reads=[cur.v(slice(0, 2), slice(16, 528))], writes=[tsm])
                    tt(dTc.v(slice(2 * g, 2 * g + 2)), tsm, H(16, 528), ALU.subtract, eng="pool")
                if ti == 0:
                    n = w - 1
                    tmp = other.v(slice(0, 2), slice(0, n))
                    tt(tmp, cur.v(slice(0, 2), slice(16, 16 + n)), invc.v(slice(0, 2), slice(0, n)), ALU.mult, eng=pe_)
                    tt(dTc.v(slice(2 * g, 2 * g + 2), slice(0, n)), tmp, H(16, 16 + n), ALU.subtract, eng=pe_)
            for oc in range(8):
                g, eh = oc // 2, oc % 2
                pp = gps(512, banks=(2, 3, 4, 5, 6, 7))
                for kc in range(2):
                    mm(pp, pwb.v(g, kc, slice(eh * 128, eh * 128 + 128)), dTc.v(2 * g + kc), kc == 0, kc == 1)
                stt(xT.v(oc, slice(t0, t0 + 512)), pp, vcol(C_PS + oc), xT.v(oc, slice(t0, t0 + 512)), ALU.mult, ALU.add)

        def final_tile(s, ti):
            t0 = 512 * ti
            rb = rstd[ti % 2]
            norm_rstd(t0, rb, ti % 2)
            ob = ost[0]
            for c in range(8):
                stt(ob.v(c), xT.v(c, slice(t0, t0 + 512)), g32(C_FIN + c), rb.v(), ALU.mult, ALU.mult)
            od = [S.dma("sp", out_d[s, c, :, t0:t0 + 512], ob.v(c).ap, "out", reads=[ob.v(c)]) for c in range(8)]
            for o in od:
                o.val = od[-1].val
            out_dmas.extend(od)
            if s + 1 < NSEQ:
                load_x(s + 1, (ti,))

        def _main():
          for s in range(NSEQ):
              if s == 0:
                  load_x(0)
              if stop == "load":
                  finish_early()
                  return
              if s == 0:
                  p1_weights()
                  for ti in range(4):
                      p1_tile_pe(ti)
              if s == 0:
                  dump("x1", xT.v())
              if stop == "p1":
                  finish_early()
                  break

              def ffn(l, between=None):
                  for sti in range(2):
                      T0 = 1024 * sti
                      for half in range(2):
                          rb = rstd[half]
                          norm_rstd(T0 + 512 * half, rb, half)
                          for c in range(8):
                              stt(hT.v(c, slice(512 * half, 512 * half + 512)), xT.v(c, slice(T0 + 512 * half, T0 + 512 * half + 512)),
                                  g32(C_NG + (l * 2 + 1) * 8 + c), rb.v(), ALU.mult, ALU.mult)
                      def load_wup(i):
                          S.dma("pool", wupb[i % NWUP].v().ap, wup_d[l, i], "wup%d" % (i % NWUP), writes=[wupb[i % NWUP].v()])

                      def load_w(idx):
                          if idx < NFC:
                              load_wup(idx)
                          elif idx < NFC + 2:
                              load_wdn(idx - NFC)

                      def load_wdn(j):
                          S.dma("pool", wdnb[j % 2].v().ap, wdn_d[l, j], "wdn%d" % (j % 2), writes=[wdnb[j % 2].v()])

                      DIST = NWUP - 1
                      prev_gate = [None]
                      for idx in range(DIST):
                          load_w(idx)
                      for i in range(NFC):
                          load_w(i + DIST)
                          wb = wupb[i % NWUP]
                          st_ = i % 2
                          pA = PS.v(slice(2048 * st_, 2048 * st_ + 1024))
                          pG = PS.v(slice(2048 * st_ + 1024, 2048 * st_ + 2048))
                          for part, pX in ((0, pA), (1, pG)):
                              for kc in range(8):
                                  for half in range(2):
                                      mm(PS.v(slice(2048 * st_ + 1024 * part + 512 * half, 2048 * st_ + 1024 * part + 512 * half + 512)),
                                         wb.v(kc, slice(128 * part, 128 * part + 128)), hT.v(kc, slice(512 * half, 512 * half + 512)), kc == 0, kc == 7)
                          yA, yG, sG = ya[i % 3], yg[st_], sg[i % 3]
                          for part, pX, yX in ((0, pA, yA), (1, pG, yG)):
                              ch = part * NFC + i
                              w0 = vcol(C_CW + (l * 3 + 0) * 44 + ch)
                              w1 = vcol(C_CW + (l * 3 + 1) * 44 + ch)
                              w2 = vcol(C_CW + (l * 3 + 2) * 44 + ch)
                              bb = vcol(C_CB + l * 44 + ch)
                              lo_ = pX_lo(st_, part)
                              act(yX.v(), pX, AF.Identity, bias=bb, scale=w2)
                              if sti == 0:
                                  cp(uh.v(ch), PS.v(slice(lo_ + 1022, lo_ + 1024)), eng="act")
                              act(yX.v(slice(1, 2)), PS.v(slice(lo_, lo_ + 1)), AF.Identity, bias=yX.v(slice(1, 2)), scale=w1)
                              if sti == 1:
                                  act(yX.v(slice(1, 2)), uh.v(ch, slice(1, 2)), AF.Identity, bias=yX.v(slice(1, 2)), scale=w0)
                                  act(yX.v(slice(0, 1)), uh.v(ch, slice(1, 2)), AF.Identity, bias=yX.v(slice(0, 1)), scale=w1)
                                  act(yX.v(slice(0, 1)), uh.v(ch, slice(0, 1)), AF.Identity, bias=yX.v(slice(0, 1)), scale=w0)
                              stt(yX.v(slice(2, 1024)), PS.v(slice(lo_ + 1, lo_ + 1023)), w1, yX.v(slice(2, 1024)), ALU.mult, ALU.add)
                              stt(yX.v(slice(2, 1024)), PS.v(slice(lo_, lo_ + 1022)), w0, yX.v(slice(2, 1024)), ALU.mult, ALU.add)
                          if prev_gate[0]:
                              prev_gate[0]()

                          def gate(i=i, yA=yA, yG=yG, sG=sG):
                              act(sG.v(), yG.v(), AF.Silu)
                              tt(actb.v(i), yA.v(), sG.v(), ALU.mult, eng="pool")
                          prev_gate[0] = gate
                      prev_gate[0]()
                      prev_gate[0] = None
                      for j in range(8):
                          wd = wdnb[j % 2]
                          for half in range(2):
                              pd = gps(512)
                              for fc in range(NFC):
                                  mm(pd, wd.v(fc), actb.v(fc, slice(512 * half, 512 * half + 512)), fc == 0, fc == NFC - 1)
                              xs = xT.v(j, slice(T0 + 512 * half, T0 + 512 * half + 512))
                              tt(xs, xs, pd, ALU.add)
                          if j + 2 < 8:
                              load_wdn(j + 2)
                      if between:
                          between(sti)

              def pX_lo(st_, part):
                  return 2048 * st_ + 1024 * part

              ffn(0)
              if s == 0:
                  dump("x2", xT.v())
              if stop == "ffn0":
                  finish_early()
                  break

              wi = {"i": 0}

              def wsm_load(src, vshape=False):
                  k = wi["i"] % NWS
                  b = wsmv[k] if vshape else wsm[k]
                  S.dma("pool", b.v().ap, src, "wsm%d" % k, writes=[b.v()])
                  wi["i"] += 1
                  return b

              PB = (0, 1, 2, 3)
              def proj_a(ti, sq_done=False, part="all"):
                  t0 = 512 * ti
                  rb = rstd[ti % 2]
                  if part in ("all", "norm"):
                      if not sq_done:
                          norm_sq(t0)
                      norm_fin(rb, ti % 2)
                  if part == "norm":
                      return
                  for c in range(8):
                      stt(hkv.v(c), xT.v(c, slice(t0, t0 + 512)), g32(C_KV + c), rb.v(), ALU.mult, ALU.mult)
                  for c in range(8):
                      stt(hq.v(c), xT.v(c, slice(t0, t0 + 512)), g32(C_NG + (1 * 2 + 0) * 8 + c), rb.v(), ALU.mult, ALU.mult)
                  if ti == 0:
                      chk("kv_a")
              def proj_b(ti, nxt=False, pre0=None, mid=None):
                  t0 = 512 * ti
                  if nxt:
                      norm_sq(t0 + 512)
                  def proj_chunks(wsrc, hsrc, dstf, maxf, pre=None):
                      pend = None
                      for j in range(8):
                          if j % 2 == 0:
                              wb = pre if (j == 0 and pre is not None) else wsm_load(wsrc[j // 2])
                          pk = gps(512, PB)
                          for kc in range(8):
                              mm(pk, wb.v(j % 2, kc), hsrc.v(kc), kc == 0, kc == 7)
                          dst = dstf(j)
                          cp(dst, pk, eng="dve")
                          nb = PT[0][j % 2].v()
                          act(nb, dst, AF.Square)
                          if pend:
                              pend()

                          def fin(j=j, nb=nb):
                              pn = gps(512, PB)
                              mm(pn, ones, nb, True, True)
                              maxf(j, pn)
                          pend = fin
                      return pend

                  def kmax(j, pn):
                      if ti == 0:
                          S.op("dve", lambda e, pn=pn, j=j: e.tensor_reduce(out=kq2.v(slice(j, j + 1)).ap, in_=pn.ap, axis=mybir.AxisListType.X, op=ALU.max),
                               reads=[pn], writes=[kq2.v(slice(j, j + 1))])
                      else:
                          S.op("dve", lambda e, pn=pn: e.tensor_reduce(out=kq2.v(slice(32, 33)).ap, in_=pn.ap, axis=mybir.AxisListType.X, op=ALU.max),
                               reads=[pn], writes=[kq2.v(slice(32, 33))])
                          tt(kq2.v(slice(j, j + 1)), kq2.v(slice(j, j + 1)), kq2.v(slice(32, 33)), ALU.max)

                  def qmax(j, pn):
                      S.op("dve", lambda e, pn=pn, j=j: e.tensor_reduce(out=kq2.v(slice(8 + j, 9 + j)).ap, in_=pn.ap, axis=mybir.AxisListType.X, op=ALU.max),
                           reads=[pn], writes=[kq2.v(slice(8 + j, 9 + j))])

                  lastk = proj_chunks(wk_d, hkv, lambda j: KT.v(j, slice(t0, t0 + 512)), kmax, pre=pre0)
                  if mid:
                      mid()
                  for qv in range(4):
                      wv_ = wsm_load(wv_d[qv], vshape=True)
                      for tk in range(4):
                          pv = gps(256, PB)
                          for kc in range(8):
                              mm(pv, hkv.v(kc, slice(128 * tk, 128 * tk + 128)), wv_.v(kc), kc == 0, kc == 7)
                          evac_copy(Vb.v(4 * ti + tk, slice(256 * qv, 256 * qv + 256)), pv)
                  lastk()
                  lastq = proj_chunks(wq_d, hq, lambda j: QT.v(j), qmax)
                  lastq()
                  tt(kq2.v(slice(16, 24)), kq2.v(slice(0, 8)), kq2.v(slice(8, 16)), ALU.mult)
                  act(kq2.v(slice(24, 32)), kq2.v(slice(16, 24)), AF.Ln)
                  act(kq2.v(slice(16, 24)), kq2.v(slice(24, 32)), AF.Exp, scale=0.5)
                  S.op("dve", lambda e: e.tensor_scalar_mul(out=kq2.v(slice(16, 24)).ap, in0=kq2.v(slice(16, 24)).ap, scalar1=-SM_SCALE),
                       reads=[kq2.v(slice(16, 24))], writes=[kq2.v(slice(16, 24))])
                  if nxt:
                      proj_a(ti + 1, sq_done=True, part="norm")
                  if s == 0 and ti == 1:
                      dump("kq2", kq2.v())
                  if ti == 0:
                      chk("kvq")

              def heads(ti):
                  nk = 4 * ti + 4
                  pendA, pendB = [], []

                  def flush_pending(pending):
                      while pending:
                          pending.pop(0)()
                  Sb = lambda c, b: 512 * (2 * c + b)
                  Ob = lambda c: 512 * (4 + c)
                  Lb = lambda c: 512 * (6 + c)
                  for head in range(8):
                      def s_mm(kt, head=head):
                          i = kt - 4 * ti
                          lo = 128 * i if i >= 0 else 0
                          for c in range(2):
                              pr = (64 * c, 64 * c + 64)
                              base = Sb(c, kt % 2)
                              mm(PS.v(slice(base + lo, base + 512)), KT.v(head, slice(128 * kt, 128 * kt + 128), p=pr), QT.v(head, slice(lo, 512), p=pr), True, i < 0)
                          if i >= 0:
                              for c in range(2):
                                  base = Sb(c, kt % 2)
                                  mm(PS.v(slice(base + lo, base + lo + 128)), ident, negmask, False, True)
                          return lo

                      def exps(kt, lo, head, cs=(0, 1)):
                          for c in cs:
                              base = Sb(c, kt % 2)
                              act(PT[c][kt % 2].v(slice(lo, 512)), PS.v(slice(base + lo, base + 512)), AF.Exp, bias=kq2.v(slice(16 + head, 17 + head)), scale=SM_SCALE)

                      los = {}
                      if head == 0:
                          los[0] = s_mm(0)
                      else:
                          los[0] = 0
                      for kt in range(nk):
                          if kt + 1 < nk:
                              los[kt + 1] = s_mm(kt + 1)
                          elif head + 1 < 8:
                              s_mm(0, head + 1)
                          lo = los[kt]
                          if not (kt == 0 and head > 0):
                              exps(kt, lo, head)
                          if kt == 2:
                              flush_pending(pendA)
                          if kt == 3:
                              flush_pending(pendB)
                          for c in range(2):
                              pt = PT[c][kt % 2].v(slice(lo, 512))
                              mm(PS.v(slice(Ob(c) + lo, Ob(c) + 512)), Vb.v(kt, slice(128 * head, 128 * head + 128)), pt, kt == 0, kt == nk - 1)
                              mm(PS.v(slice(Lb(c) + lo, Lb(c) + 512)), ones, pt, kt == 0, kt == nk - 1)
                      O0, O1 = PS.v(slice(Ob(0), Ob(0) + 512)), PS.v(slice(Ob(1), Ob(1) + 512))
                      L0, L1 = PS.v(slice(Lb(0), Lb(0) + 512)), PS.v(slice(Lb(1), Lb(1) + 512))
                      if head + 1 < 8:
                          exps(0, 0, head + 1, cs=(0,))
                      act(ft[0].v(), L0, AF.Ln)
                      cp(ft[2].v(), O0, eng="dve")
                      if head + 1 < 8:
                          exps(0, 0, head + 1, cs=(1,))
                      act(ft[1].v(), L1, AF.Ln)
                      cp(ft[3].v(), O1, eng="dve")

                      def finA(head=head):
                          act(ft[0].v(), ft[0].v(), AF.Exp, scale=-1.0)
                          act(ft[1].v(), ft[1].v(), AF.Exp, scale=-1.0)
                          tt(ft[2].v(), ft[2].v(), ft[0].v(), ALU.mult)
                          tt(ft[3].v(), ft[3].v(), ft[1].v(), ALU.mult)
                          stt(oT.v(head), ft[3].v(), neglam, ft[2].v(), ALU.mult, ALU.add)
                      pendA.append(finA)
                      pendB.append(lambda head=head: tt(osq(head), oT.v(head), oT.v(head), ALU.mult, eng="pool"))
                      if ti == 0 and head == 0:
                          chk("att1")
                      if ti == 0 and head == 1:
                          chk("att2")
                  flush_pending(pendA)
                  flush_pending(pendB)
              def osq(head):
                  return QT.v(head)

              p2bank = {}

              def part2_pe(hs):
                  for head in hs:
                      pn = PS.v(slice(512 * ((head + 4) % 8), 512 * ((head + 4) % 8) + 512))
                      p2bank[head] = pn
                      mm(pn, ones, osq(head), True, True)

              def part2_rest(hs):
                  for head in hs:
                      pn = p2bank[head]
                      fa, fb = ft[(2 * head) % 4], ft[(2 * head + 1) % 4]
                      act(fa.v(), pn, AF.Ln, bias=eps_h)
                      act(fb.v(), fa.v(), AF.Exp, scale=-0.5)
                      stt(oT.v(head), oT.v(head), gsub, fb.v(), ALU.mult, ALU.mult)
              def wo_proj(ti):
                  t0 = 512 * ti
                  for j in range(8):
                      if j % 2 == 0:
                          wb = wsm_load(wo_d[j // 2])
                      pw_ = gps(512, PB)
                      for hh in range(8):
                          mm(pw_, wb.v(j % 2, hh), oT.v(hh), hh == 0, hh == 7)
                      xs = xT.v(j, slice(t0, t0 + 512))
                      tt(xs, xs, pw_, ALU.add)
              proj_a(0)
              proj_b(0, nxt=True)
              proj_a(1, sq_done=True, part="stt")
              for ti in range(4):
                  heads(ti)
                  if ti < 3:
                      pre0 = wsm_load(wk_d[0])
                      part2_pe(range(0, 4))
                      part2_rest(range(0, 4))
                      part2_pe(range(4, 7))
                      part2_rest(range(4, 7))

                      def mid():
                          part2_pe(range(7, 8))
                          part2_rest(range(7, 8))
                      proj_b(ti + 1, nxt=(ti + 1 < 3), pre0=pre0, mid=mid)
                  else:
                      part2_pe(range(0, 4))
                      part2_rest(range(0, 4))
                      part2_pe(range(4, 8))
                      part2_rest(range(4, 8))
                  wo_proj(ti)
                  if ti + 2 < 4:
                      proj_a(ti + 2, sq_done=True, part="stt")
              if s == 0:
                  dump("x3", xT.v())
              if stop == "att":
                  finish_early()
                  break

              def between(sti, s=s):
                  for ti in (2 * sti, 2 * sti + 1):
                      final_tile(s, ti)
                  if s + 1 < NSEQ:
                      p1_weights()
                      for ti in (2 * sti, 2 * sti + 1):
                          p1_tile_pe(ti)
              ffn(1, between)

        try:
            _main()
        except _Stop:
            pass
        S.emit_all(final_wait_ops=out_dmas)
        print("ops", {e: len(S.ops[e]) for e in ENGINES}, "sems", S.nsems, "waits", S.nwaits)
    return nc


_PROG = {}


def _host_layouts(inp):
    f = lambda a: np.ascontiguousarray(np.asarray(a, dtype=np.float32))
    w_up = f(inp["w_up"])
    a = w_up[:, :, :DFF].reshape(2, 8, 128, NFC, 128)
    g = w_up[:, :, DFF:].reshape(2, 8, 128, NFC, 128)
    wup = np.stack([a, g], axis=4)
    wup = f(wup.transpose(0, 3, 2, 1, 4, 5).reshape(2, NFC, 128, 8, 256))
    wdn = f(f(inp["w_down"]).reshape(2, NFC, 128, 8, 128).transpose(0, 3, 2, 1, 4))
    pw = f(f(inp["pool_w"])[0].reshape(4, 2, 128, 256).transpose(2, 0, 1, 3))
    w_kv = f(inp["w_kv"])
    hperm = lambda w: w.reshape(1024, 2, 8, 64).transpose(0, 2, 1, 3).reshape(1024, 1024)
    wk = f(hperm(w_kv[:, :1024]).reshape(8, 128, 8, 128).transpose(2, 1, 0, 3))
    wv = f(w_kv[:, 1024:].reshape(8, 128, 4, 256).transpose(2, 1, 0, 3))
    wq = f(hperm(f(inp["wq"])[0]).reshape(8, 128, 8, 128).transpose(2, 1, 0, 3))
    wo = f(f(inp["wo"])[0].reshape(8, 128, 8, 128).transpose(2, 1, 0, 3))
    pair = lambda w: f(w.reshape(4, 2, 128, 8, 128).transpose(0, 2, 1, 3, 4))
    wk, wq, wo = pair(wk), pair(wq), pair(wo)
    vec = np.zeros((128, NV), np.float32)
    pm = lambda v: f(v).reshape(-1, 128).T
    vec[:, C_NG:C_NG + 32] = pm(f(inp["norm_g"]).reshape(-1))
    vec[:, C_KV:C_KV + 8] = pm(inp["kv_norm"])
    vec[:, C_FIN:C_FIN + 8] = pm(inp["final_norm"])
    vec[:, C_PS:C_PS + 8] = pm(f(inp["pool_scale"])[0])
    vec[:, C_SUB] = f(inp["subln_g"])[0]
    vec[:, C_CW:C_CW + 264] = pm(f(inp["conv_w"]).reshape(-1))
    vec[:, C_CB:C_CB + 88] = pm(f(inp["conv_b"]).reshape(-1))
    vec[:, C_LAM:C_LAM + 256] = f(inp["lam"])[0].reshape(1, 256)
    inv = (1.0 / np.arange(1, 17, dtype=np.float64)).astype(np.float32)
    vec[:, C_INV:C_INV + 16] = inv
    vec[:, C_INV + 16:C_INV + 32] = inv
    cst = np.zeros((128, 11, 128), np.float32)
    for gi, w in enumerate((2, 4, 8, 16)):
        cst[:, 3 + 2 * gi, :] = np.eye(128, dtype=np.float32) * (1.0 / w - 1.0)
        cst[:, 4 + 2 * gi, :] = np.eye(128, dtype=np.float32) * (1.0 / w)
    cst[:, 0, :] = np.eye(128, dtype=np.float32)
    cst[:, 1, :] = np.where(np.arange(128)[:, None] > np.arange(128)[None, :], NEG, 0.0)
    cst[:, 2, :] = 1.0
    return dict(vec=vec, cst=cst, wup=wup, wdn=wdn, pw=pw, wk=wk, wv=wv, wq=wq, wo=wo)


def kernel(**inputs):
    x = np.asarray(inputs["x"], dtype=np.float32)
    shared = _host_layouts(inputs)
    if "nc" not in _PROG:
        _PROG["nc"] = build_program()
    nc = _PROG["nc"]
    in_maps = []
    for c in range(8):
        xs = x[NSEQ * c:NSEQ * c + NSEQ]
        xT = np.ascontiguousarray(xs.transpose(0, 2, 1).reshape(NSEQ, 8, 128, SEQ))
        m = dict(shared)
        m["xT"] = xT
        in_maps.append(m)
    res = run_bass_kernel_spmd(nc, in_maps, core_ids=list(range(8)))
    outs = []
    for c in range(8):
        oT = np.asarray(res.results[c]["outT"]).reshape(NSEQ, D, SEQ)
        outs.append(oT.transpose(0, 2, 1))
    return np.ascontiguousarray(np.concatenate(outs, axis=0).astype(np.float32))
```
